# Optimizing a Trainium2 kernel written in Bass

```python
import math
import jax, jax.numpy as jnp
from jax import lax
import numpy as np

D_MODEL = 1024
BATCH = 8
SEQ = 2048
DEPTH = 2
DEC_BATCH = 128
DEC_SEQ = 4
PAST_LEN = 16384
PAGE_SIZE = 128

N_MIXERS = 2
N_RET_LAYERS = (DEPTH + 1) // 2
N_SWA_LAYERS = DEPTH // 2
RET_HEADS = 4
RET_DK = 256
RET_DV = 512
RET_CHUNK = 128
SWA_Q_HEADS = 16
SWA_KV_HEADS = 4
SWA_GROUP = SWA_Q_HEADS // SWA_KV_HEADS
SWA_HEAD_DIM = 64
WINDOW = 128
SWA_BLOCK = 128
ROPE_THETA = 10000.0
D_FF = -(-8 * D_MODEL // (3 * 256)) * 256
EPS = 1e-6

kernel_name = "retnet_swa_sink_hybrid_step"

F32 = jnp.float32


def rms_norm(x, gain=None):
    xf = x.astype(F32)
    y = xf * lax.rsqrt(jnp.mean(xf * xf, axis=-1, keepdims=True) + EPS)
    if gain is not None:
        y = y * gain.astype(F32)
    return y.astype(x.dtype)


def rope(x, pos, inv_freq):
    ang = pos[:, None] * inv_freq[None, :]
    cos = jnp.cos(ang)[:, None, :]
    sin = jnp.sin(ang)[:, None, :]
    xf = x.astype(F32)
    half = xf.shape[-1] // 2
    x1, x2 = xf[..., :half], xf[..., half:]
    return jnp.concatenate([x1 * cos - x2 * sin, x2 * cos + x1 * sin], axis=-1).astype(x.dtype)


def ret_inv_freq():
    return 1.0 / (ROPE_THETA ** jnp.linspace(0.0, 1.0, RET_DK // 2, dtype=F32))


def swa_inv_freq():
    return 1.0 / (ROPE_THETA ** (jnp.arange(0, SWA_HEAD_DIM, 2, dtype=F32) / SWA_HEAD_DIM))


def ret_log_decay():
    return jnp.log(1.0 - 2.0 ** (-5.0 - jnp.arange(RET_HEADS, dtype=F32)))


def retention_chunk(S, q, k, v, lg):
    C = q.shape[1]
    idx = jnp.arange(C, dtype=F32)
    diff = idx[:, None] - idx[None, :]
    decay = jnp.where(diff >= 0, jnp.exp(jnp.maximum(diff, 0.0)[None] * lg[:, None, None]), 0.0)
    scores = jnp.einsum('bihd,bjhd->bhij', q, k) * decay
    o = jnp.einsum('bhij,bjhe->bihe', scores, v)
    q_dec = q * jnp.exp((idx + 1.0)[:, None] * lg[None, :])[None, :, :, None]
    o = o + jnp.einsum('bihd,bhde->bihe', q_dec, S)
    k_dec = k * jnp.exp((C - 1.0 - idx)[:, None] * lg[None, :])[None, :, :, None]
    S_new = jnp.exp(C * lg)[None, :, None, None] * S + jnp.einsum('bjhd,bjhe->bhde', k_dec, v)
    return S_new, o


def retention_mixer(h, S0, w_in, w_out, pos0):
    B, T, _ = h.shape
    hk, hv = RET_HEADS * RET_DK, RET_HEADS * RET_DV
    proj = h @ w_in
    q = proj[..., :hk].reshape(B, T, RET_HEADS, RET_DK)
    k = proj[..., hk:2 * hk].reshape(B, T, RET_HEADS, RET_DK)
    v = proj[..., 2 * hk:2 * hk + hv].reshape(B, T, RET_HEADS, RET_DV)
    g = proj[..., 2 * hk + hv:]
    pos = pos0 + jnp.arange(T, dtype=F32)
    inv = ret_inv_freq()
    q = rope(q, pos, inv).astype(F32)
    k = rope(k, pos, inv).astype(F32) * (RET_DK ** -0.5)
    v = v.astype(F32)
    lg = ret_log_decay()
    C = RET_CHUNK if T % RET_CHUNK == 0 else T
    nc = T // C
    to_chunks = lambda a: jnp.moveaxis(a.reshape(B, nc, C, *a.shape[2:]), 1, 0)

    def step(S, xs):
        qc, kc, vc = xs
        return retention_chunk(S, qc, kc, vc, lg)

    S_fin, o = lax.scan(step, S0.astype(F32), (to_chunks(q), to_chunks(k), to_chunks(v)))
    o = jnp.moveaxis(o, 0, 1).reshape(B, T, RET_HEADS, RET_DV)
    o = rms_norm(o).reshape(B, T, hv)
    y = (jax.nn.silu(g.astype(F32)) * o).astype(h.dtype) @ w_out
    return y, S_fin


def swa_qkv(h, w_in, q_gain, k_gain, pos):
    B, T, _ = h.shape
    nq, nk = SWA_Q_HEADS * SWA_HEAD_DIM, SWA_KV_HEADS * SWA_HEAD_DIM
    proj = h @ w_in
    q = proj[..., :nq].reshape(B, T, SWA_Q_HEADS, SWA_HEAD_DIM)
    k = proj[..., nq:nq + nk].reshape(B, T, SWA_KV_HEADS, SWA_HEAD_DIM)
    v = proj[..., nq + nk:].reshape(B, T, SWA_KV_HEADS, SWA_HEAD_DIM)
    inv = swa_inv_freq()
    q = rope(rms_norm(q, q_gain), pos, inv)
    k = rope(rms_norm(k, k_gain), pos, inv)
    return q, k, v


def sink_attention(q, k, v, mask, sinks):
    s = jnp.einsum('bnqhgd,bnkhd->bnhgqk', q, k).astype(F32) * (SWA_HEAD_DIM ** -0.5)
    s = jnp.where(mask[None, :, None, None], s, -jnp.inf)
    sink = jnp.broadcast_to(sinks.astype(F32).reshape(SWA_KV_HEADS, SWA_GROUP)[None, None, :, :, None, None],
                            s.shape[:-1] + (1,))
    p = jax.nn.softmax(jnp.concatenate([s, sink], axis=-1), axis=-1)[..., :-1]
    return jnp.einsum('bnhgqk,bnkhd->bnqhgd', p.astype(v.dtype), v)


def swa_prompt(h, w_in, w_out, q_gain, k_gain, sinks):
    B, T, _ = h.shape
    q, k, v = swa_qkv(h, w_in, q_gain, k_gain, jnp.arange(T, dtype=F32))
    nb = T // SWA_BLOCK
    qb = q.reshape(B, nb, SWA_BLOCK, SWA_KV_HEADS, SWA_GROUP, SWA_HEAD_DIM)
    kb = k.reshape(B, nb, SWA_BLOCK, SWA_KV_HEADS, SWA_HEAD_DIM)
    vb = v.reshape(B, nb, SWA_BLOCK, SWA_KV_HEADS, SWA_HEAD_DIM)
    band = lambda a: jnp.concatenate([jnp.concatenate([jnp.zeros_like(a[:, :1]), a[:, :-1]], axis=1), a], axis=2)
    kband, vband = band(kb), band(vb)
    i = jnp.arange(SWA_BLOCK)[:, None]
    j = jnp.arange(2 * SWA_BLOCK)[None, :]
    d = SWA_BLOCK + i - j
    n = jnp.arange(nb)[:, None, None]
    mask = (d >= 0) & (d <= WINDOW) & ((n > 0) | (j >= SWA_BLOCK))
    o = sink_attention(qb, kband, vband, mask, sinks).reshape(B, T, SWA_Q_HEADS * SWA_HEAD_DIM)
    L = min(WINDOW, T)
    return o @ w_out, k[:, T - L:], v[:, T - L:]


def swa_sample(h, cache_k, cache_v, w_in, w_out, q_gain, k_gain, sinks):
    B, T, _ = h.shape
    q, k, v = swa_qkv(h, w_in, q_gain, k_gain, PAST_LEN + jnp.arange(T, dtype=F32))
    L = cache_k.shape[1]
    kall = jnp.concatenate([cache_k.astype(k.dtype), k], axis=1)
    vall = jnp.concatenate([cache_v.astype(v.dtype), v], axis=1)
    i = jnp.arange(T)[:, None]
    j = jnp.arange(L + T)[None, :]
    d = i - j + L
    mask = ((d >= 0) & (d <= WINDOW))[None]
    qb = q.reshape(B, 1, T, SWA_KV_HEADS, SWA_GROUP, SWA_HEAD_DIM)
    o = sink_attention(qb, kall[:, None], vall[:, None], mask, sinks).reshape(B, T, SWA_Q_HEADS * SWA_HEAD_DIM)
    return o @ w_out, kall[:, T:], vall[:, T:]


def swiglu(h, w_in, w_out):
    gu = h @ w_in
    g, u = gu[..., :D_FF], gu[..., D_FF:]
    return (jax.nn.silu(g.astype(F32)) * u.astype(F32)).astype(h.dtype) @ w_out


def setup_inputs(seed: int = 0) -> dict:
    key = jax.random.key(seed)
    ks = jax.random.split(key, 16)
    nrm = lambda k, shape, scale: jax.random.normal(k, shape, F32) * scale
    ret_in_w = 2 * RET_HEADS * RET_DK + 2 * RET_HEADS * RET_DV
    swa_in_w = (SWA_Q_HEADS + 2 * SWA_KV_HEADS) * SWA_HEAD_DIM
    buf = min(WINDOW, PAST_LEN)
    return {
        "x_prompt": nrm(ks[0], (BATCH, SEQ, D_MODEL), 1.0),
        "x_sample": nrm(ks[1], (DEC_BATCH, DEC_SEQ, D_MODEL), 1.0),
        "state_ret": nrm(ks[2], (N_RET_LAYERS, DEC_BATCH, RET_HEADS, RET_DK, RET_DV), 0.3),
        "cache_swa_k": nrm(ks[3], (N_SWA_LAYERS, DEC_BATCH, buf, SWA_KV_HEADS, SWA_HEAD_DIM), 1.0),
        "cache_swa_v": nrm(ks[4], (N_SWA_LAYERS, DEC_BATCH, buf, SWA_KV_HEADS, SWA_HEAD_DIM), 1.0),
        "norm_mix": 1.0 + nrm(ks[5], (DEPTH, D_MODEL), 0.02),
        "norm_ffn": 1.0 + nrm(ks[6], (DEPTH, D_MODEL), 0.02),
        "w_ret_in": nrm(ks[7], (N_RET_LAYERS, D_MODEL, ret_in_w), D_MODEL ** -0.5),
        "w_ret_out": nrm(ks[8], (N_RET_LAYERS, RET_HEADS * RET_DV, D_MODEL), (RET_HEADS * RET_DV) ** -0.5),
        "w_swa_in": nrm(ks[9], (N_SWA_LAYERS, D_MODEL, swa_in_w), D_MODEL ** -0.5),
        "w_swa_out": nrm(ks[10], (N_SWA_LAYERS, SWA_Q_HEADS * SWA_HEAD_DIM, D_MODEL), (SWA_Q_HEADS * SWA_HEAD_DIM) ** -0.5),
        "swa_q_norm": 1.0 + nrm(ks[11], (N_SWA_LAYERS, SWA_HEAD_DIM), 0.02),
        "swa_k_norm": 1.0 + nrm(ks[12], (N_SWA_LAYERS, SWA_HEAD_DIM), 0.02),
        "swa_sinks": nrm(ks[13], (N_SWA_LAYERS, SWA_Q_HEADS), 1.0),
        "w_ffn_in": nrm(ks[14], (DEPTH, D_MODEL, 2 * D_FF), D_MODEL ** -0.5),
        "w_ffn_out": nrm(ks[15], (DEPTH, D_FF, D_MODEL), D_FF ** -0.5),
    }


def reference(x_prompt, x_sample, state_ret, cache_swa_k, cache_swa_v, norm_mix, norm_ffn,
              w_ret_in, w_ret_out, w_swa_in, w_swa_out, swa_q_norm, swa_k_norm, swa_sinks,
              w_ffn_in, w_ffn_out):
    xp, xs = x_prompt, x_sample
    ret_p, ret_s, kp, vp, ksn, vsn = [], [], [], [], [], []
    for layer in range(DEPTH):
        hp = rms_norm(xp, norm_mix[layer])
        hs = rms_norm(xs, norm_mix[layer])
        if layer % N_MIXERS == 0:
            r = layer // N_MIXERS
            S0 = jnp.zeros((xp.shape[0], RET_HEADS, RET_DK, RET_DV), F32)
            yp, Sp = retention_mixer(hp, S0, w_ret_in[r], w_ret_out[r], 0.0)
            ys, Ss = retention_mixer(hs, state_ret[r], w_ret_in[r], w_ret_out[r], float(PAST_LEN))
            ret_p.append(Sp)
            ret_s.append(Ss)
        else:
            a = layer // N_MIXERS
            yp, k_p, v_p = swa_prompt(hp, w_swa_in[a], w_swa_out[a], swa_q_norm[a], swa_k_norm[a], swa_sinks[a])
            ys, k_s, v_s = swa_sample(hs, cache_swa_k[a], cache_swa_v[a], w_swa_in[a], w_swa_out[a],
                                      swa_q_norm[a], swa_k_norm[a], swa_sinks[a])
            kp.append(k_p)
            vp.append(v_p)
            ksn.append(k_s)
            vsn.append(v_s)
        xp = xp + yp
        xs = xs + ys
        xp = xp + swiglu(rms_norm(xp, norm_ffn[layer]), w_ffn_in[layer], w_ffn_out[layer])
        xs = xs + swiglu(rms_norm(xs, norm_ffn[layer]), w_ffn_in[layer], w_ffn_out[layer])
    new_state_ret_p = jnp.stack(ret_p, 0)
    new_state_ret_s = jnp.stack(ret_s, 0)
    new_cache_swa_k_p = jnp.stack(kp, 0)
    new_cache_swa_v_p = jnp.stack(vp, 0)
    new_cache_swa_k_s = jnp.stack(ksn, 0)
    new_cache_swa_v_s = jnp.stack(vsn, 0)
    return (xp, xs, new_state_ret_p, new_state_ret_s, new_cache_swa_k_p, new_cache_swa_v_p, new_cache_swa_k_s, new_cache_swa_v_s)
```

```python
from contextlib import ExitStack
import numpy as np
import concourse.bass as bass
import concourse.mybir as mybir
from concourse.bass_utils import run_bass_kernel_spmd

F32 = mybir.dt.float32
BF16 = mybir.dt.bfloat16
AF = mybir.ActivationFunctionType
ALU = mybir.AluOpType
AX = mybir.AxisListType

ENGS = ("pe", "act", "dve", "pool", "sp")

D = 1024
SEQ = 2048
NT = 16
NS = 64
NB = 16
TS = 4
PAST = 16384
RH, RDK, RDV = 4, 256, 512
QH, KVH, HD = 16, 4, 64
DFF = 2816
NU = DFF // 128
EPS = 1e-6
TOKS = SEQ + NS


class Op:
    __slots__ = ("eng", "emit", "deps", "idx", "sig", "sigcnt", "slot", "cnt", "is_dma", "waits", "pre_slot_wait", "seq", "dur", "tbl", "reg")

    def __init__(self, eng, emit):
        self.eng = eng
        self.emit = emit
        self.deps = []
        self.idx = -1
        self.sig = False
        self.sigcnt = 0
        self.slot = None
        self.cnt = 0
        self.is_dma = False
        self.waits = []
        self.pre_slot_wait = 0
        self.seq = 0
        self.dur = 0.3
        self.tbl = None
        self.reg = 0


class Prog:
    def __init__(self, nc):
        self.nc = nc
        self.ops = {e: [] for e in ENGS}
        self.last_w = {}
        self.readers = {}
        self.slot_cnt = {}
        self.stack = ExitStack()
        self.sems = {}
        self.slot_sems = {}
        self.inherit = {}
        self.keys_of = {}

    def sbuf(self, name, shape, dtype):
        return self.stack.enter_context(self.nc.sbuf_tensor(name, list(shape), dtype))

    def psum(self, name, shape, dtype):
        return self.stack.enter_context(self.nc.psum_tensor(name, list(shape), dtype))

    @staticmethod
    def _name(k):
        return k[0] if isinstance(k, tuple) else k

    def _first_touch(self, k, deps):
        n = self._name(k)
        ks = self.keys_of.setdefault(n, set())
        if k not in ks:
            ks.add(k)
            inh = self.inherit.get(n)
            if inh:
                deps.extend(inh)

    def all_ops_of(self, name):
        out = []
        for k in self.keys_of.get(name, ()):
            w = self.last_w.get(k)
            if w is not None:
                out.append(w)
            out.extend(self.readers.get(k, ()))
        out.extend(self.inherit.get(name, ()))
        return out

    def _track(self, op, reads, writes):
        deps = op.deps
        for k in reads:
            self._first_touch(k, deps)
            w = self.last_w.get(k)
            if w is not None:
                deps.append(w)
            self.readers.setdefault(k, []).append(op)
        for k in writes:
            self._first_touch(k, deps)
            w = self.last_w.get(k)
            if w is not None:
                deps.append(w)
            rs = self.readers.get(k)
            if rs:
                deps.extend(rs)
            self.last_w[k] = op
            self.readers[k] = []

    def op(self, eng, emit, reads=(), writes=(), after=(), dur=0.3, tbl=None):
        o = Op(eng, emit)
        o.dur = dur
        o.tbl = tbl
        self.nseq = getattr(self, "nseq", 0) + 1
        o.seq = self.nseq
        o.reg = getattr(self, "region", 0)
        o.idx = len(self.ops[eng])
        self.ops[eng].append(o)
        o.deps.extend(after)
        self._track(o, reads, writes)
        return o

    def dma(self, eng, out, in_, slot, reads=(), writes=(), new_gen=True, after=(), **kw):
        def emit(e, out=out, in_=in_, kw=kw):
            return e.dma_start(out=out, in_=in_, **kw)
        o = Op(eng, emit)
        o.is_dma = True
        self.nseq = getattr(self, "nseq", 0) + 1
        o.seq = self.nseq
        try:
            nb = 1
            for d_ in out.shape:
                nb *= d_
            o.dur = 2.0 + nb * 4 / 150e3
        except Exception:
            o.dur = 3.0
        o.reg = getattr(self, "region", 0)
        o.idx = len(self.ops[eng])
        self.ops[eng].append(o)
        o.deps.extend(after)
        c = self.slot_cnt.get(slot, 0)
        if new_gen:
            o.pre_slot_wait = c
        o.slot = slot
        o.cnt = c + 1
        self.slot_cnt[slot] = c + 1
        self._track(o, reads, writes)
        return o

    def check_deadlock(self):
        ptr = {e: 0 for e in ENGS}
        progress = True
        while progress:
            progress = False
            for e in ENGS:
                ops = self.ops[e]
                while ptr[e] < len(ops):
                    o = ops[ptr[e]]
                    ok = True
                    for d in o.deps:
                        if d is o:
                            continue
                        if d.eng == e:
                            if d.idx >= o.idx:
                                raise RuntimeError("forward same-engine dep")
                            continue
                        if d.idx >= ptr[d.eng]:
                            ok = False
                            break
                    if not ok:
                        break
                    ptr[e] += 1
                    progress = True
        stuck = {e: (ptr[e], len(self.ops[e])) for e in ENGS if ptr[e] < len(self.ops[e])}
        if stuck:
            raise RuntimeError(f"static deadlock: {stuck}")

    def schedule(self, region=1, window=600, fixed_extra=()):
        import heapq
        full = {e: list(self.ops[e]) for e in ENGS}
        sel = {e: [o for o in full[e] if o.reg == region] for e in ENGS}
        if not any(sel.values()):
            return
        keep = set(id(o) for e in ENGS for o in sel[e])
        allops = [o for e in ENGS for o in sel[e]]
        saved_ops = self.ops
        self.ops = sel
        succ = {id(o): [] for o in allops}
        ndep = {}
        for o in allops:
            ds = {id(d): d for d in o.deps if d is not o and id(d) in keep}
            ndep[id(o)] = len(ds)
            for d in ds.values():
                succ[id(d)].append(o)
        ready_t = {id(o): 0.0 for o in allops}
        fixed = {"sp", "pool"} | set(fixed_extra)
        pending = {e: list(self.ops[e]) for e in ENGS}
        ptr = {e: 0 for e in ENGS}
        avail = {e: [] for e in ENGS}
        opmap = {id(o): o for o in allops}
        for o in allops:
            if ndep[id(o)] == 0:
                heapq.heappush(avail[o.eng], (o.seq, id(o)))
        t_eng = {e: 0.0 for e in ENGS}
        last_tbl = [None]
        order = {e: [] for e in ENGS}
        done = 0
        total = len(allops)
        minseq = {e: 0 for e in ENGS}
        events = []
        finished = set()
        while done < total:
            progressed = False
            for e in ENGS:
                if not avail[e]:
                    continue
                if e in fixed:
                    nxt = pending[e][ptr[e]] if ptr[e] < len(pending[e]) else None
                    if nxt is None or ndep[id(nxt)] != 0:
                        continue
                    cand = nxt
                    avail[e] = [x for x in avail[e] if x[1] != id(cand)]
                    heapq.heapify(avail[e])
                else:
                    now = t_eng[e]
                    lo = avail[e][0][0]
                    best = None
                    for (sq_, i_) in sorted(avail[e])[:24]:
                        o_ = opmap[i_]
                        if sq_ > lo + window:
                            break
                        r_ = max(ready_t[i_], now)
                        pen = 0.0
                        if e == "act" and o_.tbl is not None and last_tbl[0] is not None and o_.tbl != last_tbl[0]:
                            pen = 1.3
                        key = (r_ + pen, sq_)
                        if best is None or key < best[0]:
                            best = (key, o_)
                    cand = best[1]
                    avail[e].remove((cand.seq, id(cand)))
                    heapq.heapify(avail[e])
                st = max(t_eng[e], ready_t[id(cand)])
                if e == "act" and cand.tbl is not None:
                    if last_tbl[0] is not None and cand.tbl != last_tbl[0]:
                        st += 1.3
                    last_tbl[0] = cand.tbl
                if cand.is_dma:
                    t_eng[e] = st + 0.1
                    fin = st + cand.dur
                else:
                    t_eng[e] = st + cand.dur
                    fin = t_eng[e]
                order[e].append(cand)
                if e in fixed:
                    ptr[e] += 1
                done += 1
                progressed = True
                for s_ in succ[id(cand)]:
                    lat = 0.05 if s_.eng == cand.eng else 0.25
                    ready_t[id(s_)] = max(ready_t[id(s_)], fin + lat)
                    ndep[id(s_)] -= 1
                    if ndep[id(s_)] == 0:
                        heapq.heappush(avail[s_.eng], (s_.seq, id(s_)))
            if not progressed:
                raise RuntimeError("scheduler stuck")
        self.ops = saved_ops
        for e in ENGS:
            assert len(order[e]) == len(sel[e])
            if not sel[e]:
                continue
            it = iter(order[e])
            self.ops[e] = [next(it) if id(o) in keep else o for o in full[e]]
            for i, o in enumerate(self.ops[e]):
                o.idx = i
        self.est_time = max(t_eng.values())

    def build(self, resched=True):
        nc = self.nc
        if resched:
            for args_ in getattr(self, "sched_regions", ((1, 600),)):
                self.schedule(*args_)
        self.check_deadlock()
        for e in ENGS:
            known = {}
            for o in self.ops[e]:
                need = {}
                if o.is_dma and o.pre_slot_wait > 0:
                    need[("slot", o.slot)] = o.pre_slot_wait
                for d in o.deps:
                    if d is o:
                        continue
                    if d.is_dma:
                        k = ("slot", d.slot)
                        v = d.cnt
                    else:
                        if d.eng == "pe" and e == "pe" and not o.is_dma:
                            continue
                        k = ("eng", d.eng)
                        v = d.idx
                        if d.eng == e and d.idx >= o.idx:
                            raise RuntimeError("forward same-engine dep")
                    if need.get(k, -1) < v:
                        need[k] = v
                o.waits = []
                for k, v in need.items():
                    if known.get(k, -1) >= v:
                        continue
                    known[k] = v
                    o.waits.append((k, v))
                    if k[0] == "eng":
                        self.ops[k[1]][v].sig = True
                o.deps = None
        for e in ENGS:
            c = 0
            for o in self.ops[e]:
                if o.sig:
                    c += 1
                o.sigcnt = c
            assert c < 60000, (e, c)
        for e in ENGS:
            self.sems[e] = self.stack.enter_context(nc.semaphore("s_" + e))
        for s in self.slot_cnt:
            self.slot_sems[s] = self.stack.enter_context(nc.semaphore("d_" + str(s)))
        block = self.stack.enter_context(nc.Block())
        final_slots = dict(self.slot_cnt)
        last_sig = {e: (self.ops[e][-1].sigcnt if self.ops[e] else 0) for e in ENGS}

        def run(e, eng):
            for o in self.ops[e]:
                for (k, v) in o.waits:
                    if k[0] == "eng":
                        eng.wait_ge(self.sems[k[1]], self.ops[k[1]][v].sigcnt)
                    else:
                        eng.wait_ge(self.slot_sems[k[1]], 16 * v)
                ins = o.emit(eng)
                if o.is_dma:
                    ins.then_inc(self.slot_sems[o.slot], 16)
                elif o.sig:
                    ins.then_inc(self.sems[e], 1)
            if e == "sp":
                for s, c in final_slots.items():
                    eng.wait_ge(self.slot_sems[s], 16 * c)

        @block.tensor
        def _(eng):
            run("pe", eng)

        @block.scalar
        def _(eng):
            run("act", eng)

        @block.vector
        def _(eng):
            run("dve", eng)

        @block.gpsimd
        def _(eng):
            run("pool", eng)

        @block.sync
        def _(eng):
            run("sp", eng)

    def close(self):
        self.stack.close()


def _ret_lg():
    return np.log(np.float32(1.0) - np.float32(2.0) ** (np.float32(-5.0) - np.arange(RH, dtype=np.float32))).astype(np.float32)


def build_consts():
    c = {}
    lg = _ret_lg()
    inv = (np.float32(1.0) / np.power(np.float32(10000.0), np.linspace(0.0, 1.0, RDK // 2, dtype=np.float32))).astype(np.float32)
    pos = np.concatenate([np.arange(SEQ, dtype=np.float32),
                          np.tile(np.float32(PAST) + np.arange(TS, dtype=np.float32), NB)]).astype(np.float32)
    ang = (inv[:, None] * pos[None, :]).astype(np.float32)
    c["c_rcos"] = np.cos(ang).astype(np.float32)
    c["c_rsin"] = np.sin(ang).astype(np.float32)
    i = np.arange(128, dtype=np.float32)
    diff = i[None, :] - i[:, None]
    decT = np.zeros((128, RH, 128), np.float32)
    for h in range(RH):
        decT[:, h, :] = np.where(diff >= 0, np.exp(np.maximum(diff, 0.0) * lg[h]), 0.0)
    c["c_decT"] = decT.astype(np.float32)
    qrow = np.zeros((RH, 128 + NS), np.float32)
    tt = np.tile(np.arange(TS, dtype=np.float32), NB)
    for h in range(RH):
        qrow[h, :128] = np.exp((i + 1.0) * lg[h])
        qrow[h, 128:] = np.exp((tt + 1.0) * lg[h])
    c["c_qrow"] = np.broadcast_to(qrow[None], (128, RH, 128 + NS)).astype(np.float32).copy()
    kcol = np.zeros((128, 2 * RH), np.float32)
    for h in range(RH):
        kcol[:, h] = np.exp((127.0 - i) * lg[h])
        kcol[:NS, RH + h] = np.exp((TS - 1.0 - tt) * lg[h])
    c["c_kcol"] = kcol
    bj = np.arange(NS) // TS
    tj = (np.arange(NS) % TS).astype(np.float32)
    same = (bj[:, None] == bj[None, :])
    dd = tj[None, :] - tj[:, None]
    sdec = np.zeros((NS, RH, NS), np.float32)
    for h in range(RH):
        sdec[:, h, :] = np.where(same & (dd >= 0), np.exp(np.maximum(dd, 0.0) * lg[h]), 0.0)
    c["c_sdec"] = sdec
    cm = (np.arange(NB)[:, None] == bj[None, :]).astype(np.float32)
    c["c_colmask"] = np.broadcast_to(cm[None], (128, NB, NS)).astype(np.float32).copy()
    c["c_rowmask"] = (bj[:, None] == np.arange(NB)[None, :]).astype(np.float32)
    c["c_ident"] = np.eye(128, dtype=np.float32)
    sinv = (np.float32(1.0) / np.power(np.float32(10000.0), np.arange(0, HD, 2, dtype=np.float32) / np.float32(HD))).astype(np.float32)
    spos = np.zeros((128, NT + 1), np.float32)
    for b in range(NT):
        spos[:, b] = b * 128 + np.arange(128)
    spos[:NS, NT] = np.float32(PAST) + tt
    sang = (spos[:, :, None] * sinv[None, None, :]).astype(np.float32)
    c["c_scos"] = np.cos(sang).astype(np.float32)
    c["c_ssin"] = np.sin(sang).astype(np.float32)
    kk = np.arange(128)
    bm = np.zeros((128, 2, 128), np.float32)
    bm[:, 0, :] = (kk[:, None] >= kk[None, :])
    bm[:, 1, :] = (kk[:, None] <= kk[None, :])
    c["c_band"] = bm
    c["c_cmask"] = (kk[:, None] >= np.arange(TS)[None, :]).astype(np.float32)
    c["c_nmask"] = (same & (dd >= 0)).astype(np.float32)
    return c


CONST_SHAPES = None


class Builder:
    swa_blocks = NT
    dbg = frozenset()

    def __init__(self, stage=99, debug=False):
        self.stage = stage
        self.debug = debug
        nc = bass.Bass("TRN2", target_bir_lowering=False)
        self.nc = nc
        self.P = Prog(nc)
        self.din = {}
        self.dout = {}
        self.uid = 0

    def inp(self, name, shape):
        self.din[name] = self.nc.dram_tensor(name, list(shape), F32, kind="ExternalInput").ap()
        return self.din[name]

    def outp(self, name, shape):
        self.dout[name] = self.nc.dram_tensor(name, list(shape), F32, kind="ExternalOutput").ap()
        return self.dout[name]

    def init_mem(self, total_f32):
        self.R = self.P.sbuf("R", [128, total_f32], F32)
        self.R_total = total_f32 * 4
        self.live = {}
        self.freed = []
        self.top = 0

    def alloc(self, name, free_shape, dtype, at=None):
        esz = 2 if dtype == BF16 else 4
        n = int(np.prod(free_shape))
        size = (n * esz + 63) // 64 * 64
        if at is None:
            off = self._find(size)
        else:
            off = at
        assert off + size <= self.R_total, ("SBUF overflow", name, off, size, self.R_total)
        for (o2, s2) in self.live.values():
            assert off + size <= o2 or o2 + s2 <= off, ("overlap live", name)
        self.live[name] = (off, size)
        self.peak = max(getattr(self, "peak", 0), off + size)
        inh = []
        for (o2, s2, n2) in self.freed:
            if not (off + size <= o2 or o2 + s2 <= off):
                inh.extend(self.P.all_ops_of(n2))
        if inh:
            self.P.inherit[name] = inh
        assert name not in self.P.keys_of, ("buffer name reused", name)
        v = self.R[:, off // 4:(off + size) // 4]
        if dtype == BF16:
            v = v.bitcast(BF16)
        v = v[:, 0:n]
        if len(free_shape) == 2:
            v = v.rearrange("p (a b) -> p a b", a=free_shape[0])
        elif len(free_shape) == 3:
            v = v.rearrange("p (a b c) -> p a b c", a=free_shape[0], b=free_shape[1])
        elif len(free_shape) == 4:
            v = v.rearrange("p (a b c d) -> p a b c d", a=free_shape[0], b=free_shape[1], c=free_shape[2])
        return v

    def _find(self, size):
        iv = sorted(self.live.values())
        pos = 0
        for (o, s) in iv:
            if o - pos >= size:
                return pos
            pos = max(pos, o + s)
        return pos

    def free(self, *names):
        for name in names:
            off, size = self.live.pop(name)
            self.freed.append((off, size, name))

    @staticmethod
    def _fsz(ap):
        n = 1
        for d_ in ap.shape[1:]:
            n *= d_
        return n

    def mm(self, out, lhsT, rhs, start, stop, reads, writes):
        return self.P.op("pe", lambda e: e.matmul(out, lhsT=lhsT, rhs=rhs, start=start, stop=stop), reads, writes,
                         dur=0.06 + self._fsz(out) / 1700.0)

    def tr(self, out, in_, ident, reads, writes):
        return self.P.op("pe", lambda e: e.transpose(out=out, in_=in_, identity=ident), reads, writes, dur=0.11)

    _TBL = {"Silu": "silu", "Sqrt": "sqrt", "Exp": "exp", "Ln": "exp"}

    def act(self, out, in_, func, reads, writes, **kw):
        tbl = self._TBL.get(getattr(func, "name", str(func)).split(".")[-1])
        return self.P.op("act", lambda e: e.activation(out=out, in_=in_, func=func, **kw), reads, writes,
                         dur=0.22 + self._fsz(out) / 1000.0, tbl=tbl)

    def _vd(self, out, eng):
        return (0.12 + self._fsz(out) / 900.0) * (2.0 if eng == "pool" else 1.0)

    def tt(self, out, in0, in1, op, reads, writes, eng="dve"):
        return self.P.op(eng, lambda e: e.tensor_tensor(out=out, in0=in0, in1=in1, op=op), reads, writes, dur=self._vd(out, eng))

    def ts(self, out, in0, s1, s2, op0, op1, reads, writes, eng="dve"):
        if op1 is None:
            return self.P.op(eng, lambda e: e.tensor_scalar(out=out, in0=in0, scalar1=s1, scalar2=None, op0=op0), reads, writes,
                             dur=self._vd(out, eng))
        return self.P.op(eng, lambda e: e.tensor_scalar(out=out, in0=in0, scalar1=s1, scalar2=s2, op0=op0, op1=op1), reads, writes,
                         dur=self._vd(out, eng))

    def stt(self, out, in0, scalar, in1, op0, op1, reads, writes, eng="dve"):
        return self.P.op(eng, lambda e: e.scalar_tensor_tensor(out=out, in0=in0, scalar=scalar, in1=in1, op0=op0, op1=op1), reads, writes,
                         dur=self._vd(out, eng))

    def cp(self, out, in_, reads, writes, eng="dve"):
        return self.P.op(eng, lambda e: e.tensor_copy(out=out, in_=in_), reads, writes, dur=self._vd(out, eng))

    def recip(self, out, in_, reads, writes):
        return self.P.op("dve", lambda e: e.reciprocal(out=out, in_=in_), reads, writes, dur=self._vd(out, "dve"))

    def memset(self, ap, val, writes, eng="dve"):
        return self.P.op(eng, lambda e: e.memset(ap, val), (), writes, dur=0.1 + self._fsz(ap) / 2000.0)

    def reduce(self, out, in_, op, reads, writes):
        return self.P.op("dve", lambda e: e.tensor_reduce(out=out, in_=in_, axis=AX.X, op=op), reads, writes,
                         dur=0.12 + self._fsz(in_) / 900.0)

    def init_psum(self):
        self.PS = self.P.psum("PS", [128, 8, 512], F32)
        self.ps_next = 0
        self.ps_pinned = set()

    def bank(self, n=1):
        def consumed(b):
            k = ("ps", b)
            return self.P.last_w.get(k) is None or len(self.P.readers.get(k, ())) > 0
        for _ in range(16):
            i = self.ps_next
            if i + n <= 8 and all(((i + j) not in self.ps_pinned) and consumed(i + j) for j in range(n)):
                self.ps_next = (i + n) % 8
                return i, [("ps", i + j) for j in range(n)]
            self.ps_next = (i + 1) % 8
        raise RuntimeError("no free PSUM bank (all pending consumption)")

    def psf(self, i, n=1):
        return self.PS[:, i:i + n, :].rearrange("p a b -> p (a b)")

    def psb(self, i):
        return self.PS[:, i, :].bitcast(BF16)


W_ELEMS = 24576
FFN_GROUPS = [4, 4, 4, 4, 3, 3]
R_F32 = 53150
WARM_MM = 16
SCHED_REGIONS = ((1, 600), (2, 300, ("pe",)))


class Mega(Builder):
    def plan_weights(self):
        d = self.din
        pieces = []

        def colpiece(name, w, c0, n):
            pieces.append((name, [8, n], w[:, c0:c0 + n].rearrange("(c p) n -> p c n", p=128)))

        def rowpiece(name, w, r0, nchunk):
            pieces.append((name, [nchunk, D], w[r0:r0 + nchunk * 128, :].rearrange("(c p) n -> p c n", p=128)))

        wri, wro = d["w_ret_in"], d["w_ret_out"]
        only = self.only
        for h in range(RH if "ret" in only else 0):
            colpiece(f"rq{h}", wri, h * RDK, RDK)
            colpiece(f"rk{h}", wri, RH * RDK + h * RDK, RDK)
            colpiece(f"rv{h}", wri, 2 * RH * RDK + h * RDV, RDV)
            colpiece(f"rg{h}", wri, 2 * RH * RDK + RH * RDV + h * RDV, RDV)
            rowpiece(f"ro{h}", wro, h * RDV, RDV // 128)

        def ffn(l):
            wi, wo = d[f"w_ffn_in{l}"], d[f"w_ffn_out{l}"]
            u0 = 0
            for gi, G in enumerate(FFN_GROUPS):
                colpiece(f"fg{l}_{gi}", wi, u0 * 128, G * 128)
                colpiece(f"fu{l}_{gi}", wi, DFF + u0 * 128, G * 128)
                rowpiece(f"fo{l}_{gi}", wo, u0 * 128, G)
                u0 += G
        if "ffn0" in only:
            ffn(0)
        if "swa" in only:
            for j in range(3):
                colpiece(f"si{j}", d["w_swa_in"], j * 512, 512)
            rowpiece("so", d["w_swa_out"], 0, 8)
        if "ffn1" in only:
            ffn(1)
        self.W = {}
        self.wkey = {}
        live = []
        ptr = 0
        self.evictions = []
        for i, (name, shp, src) in enumerate(pieces):
            n = int(np.prod(shp))
            if ptr + n > W_ELEMS:
                ptr = 0
            ev = [x for x in live if not (ptr + n <= x[0] or x[0] + x[1] <= ptr)]
            live = [x for x in live if x not in ev]
            live.append((ptr, n, name))
            v = self.WA[:, ptr:ptr + n].rearrange("p (a b) -> p a b", a=shp[0])
            self.W[name] = v
            key = ("W", name)
            self.wkey[name] = key
            op = self.P.dma("pool", v, src, slot=("w", i % 8), writes=[key])
            for x in ev:
                self.evictions.append((op, x[2]))
            ptr += n

    def resolve_evictions(self):
        for op, name in self.evictions:
            key = ("W", name)
            w = self.P.last_w.get(key)
            rs = self.P.readers.get(key, [])
            assert op.deps is not None
            op.deps.extend(rs)
            if w is not None:
                op.deps.append(w)

    def norm_setup(self):
        P = self.P
        self.gainT = self.alloc("gainT", [4, 8], F32)
        srcs = [("norm_mix", 0), ("norm_ffn", 0), ("norm_mix", 1), ("norm_ffn", 1)]
        for i, (nm, l) in enumerate(srcs):
            P.dma("sp", self.gainT[:, i, :], self.din[nm][l:l + 1, :].rearrange("o (c p) -> p (o c)", p=128), slot="gT",
                  writes=["gainT"], new_gen=False, allow_slow_non_contiguous=True)
        self.hbs = [self.alloc(f"hb{i}", [D], BF16) for i in range(2)]
        self.nst = self.alloc("nst", [2, 4], F32)
        self.norm_cnt = 0
        self.norm_ctx = {}

    def norm_tile(self, gi, t, part="ab", lnexp=False):
        n = 128 if t < NT else NS
        if "a" in part:
            i = self.norm_cnt % 2
            self.norm_cnt += 1
            self.norm_ctx[t] = i
            hb, hk = self.hbs[i], f"hb{i}"
            st = self.nst[:, i, :]
            sk = ("nst", i)
            xk = ("X", t)
            x = self.X[0:n, t, :]
            self.act(hb[0:n, :], x, AF.Square, [xk], [hk, sk], accum_out=st[0:n, 0:1])
            if lnexp:
                self.act(st[0:n, 1:2], st[0:n, 0:1], AF.Ln, [sk, "epsb"], [sk], scale=1.0 / D, bias=self.epsb[0:n, 0:1])
                self.act(st[0:n, 2:3], st[0:n, 1:2], AF.Exp, [sk], [sk], scale=-0.5)
            else:
                self.act(st[0:n, 1:2], st[0:n, 0:1], AF.Sqrt, [sk, "epsb"], [sk], scale=1.0 / D, bias=self.epsb[0:n, 0:1])
                self.recip(st[0:n, 2:3], st[0:n, 1:2], [sk], [sk])
            self.ts(hb[0:n, :], x, st[0:n, 2:3], None, ALU.mult, None, [xk, sk], [hk])
        if "b" in part:
            i = self.norm_ctx.pop(t)
            hb, hk = self.hbs[i], f"hb{i}"
            bi, bk = self.bank()
            pb = self.psb(bi).rearrange("p (a b) -> p a b", a=8)
            for c in range(8):
                self.tr(pb[:, c, 0:n], hb[0:n, c * 128:(c + 1) * 128], self.ident[0:n, 0:n], [hk, "ident"], bk)
            self.tt(self.HT[:, :, t * 128:t * 128 + n], pb[:, :, 0:n], self.gainT[:, gi, :].unsqueeze(2).broadcast_to([128, 8, n]), ALU.mult,
                    bk + ["gainT"], [("HT", t)])

    def retention(self, next_norm):
        P = self.P
        lg = _ret_lg()
        g128 = [float(np.exp(np.float32(128.0) * lg[h])) for h in range(RH)]
        g4 = [float(np.exp(np.float32(4.0) * lg[h])) for h in range(RH)]
        A = self.alloc
        rcos, rsin = self.rcos, self.rsin
        decT = A("decT", [128], F32); qrow = A("qrow", [128 + NS], F32); kcol = A("kcol", [2 * RH], F32)
        sdec = A("sdec", [NS], F32); rowmask = A("rowmask", [NB], F32)
        c = self.din
        P.dma("sp", kcol, c["c_kcol"][:, :], slot="c4", writes=["kcol"])
        P.dma("sp", rowmask[0:NS], c["c_rowmask"][:, :], slot="c6", writes=["rowmask"])
        qr = [A(f"qr{i}", [2, 512], BF16) for i in range(2)]
        kr = [A(f"kr{i}", [2, 512], BF16) for i in range(2)]
        qd = [A(f"qd{i}", [2, 128], BF16) for i in range(2)]
        ta = A("ta", [512], F32); tb = A("tb", [512], F32)
        vc = [A(f"vc{i}", [512], BF16) for i in range(2)]; sg4 = A("sg4", [4, 512], BF16)
        sTm = [A(f"sTm{i}", [128], BF16) for i in range(2)]
        ktok = [A(f"ktok{i}", [256], BF16) for i in range(2)]
        gated = A("gated", [512], BF16); oTg = A("oTg", [4, 128], BF16)
        Sf = A("Sf", [2, 512], F32); Sb = A("Sb", [2, 512], BF16)
        rst = A("rst", [2, 4], F32)
        qr_s = A("qr_s", [2, NS], BF16); kr_s = A("kr_s", [2, NS], BF16); qd_s = A("qd_s", [2, NS], BF16)
        v_s = A("v_s", [512], BF16); sg_s = A("sg_s", [512], BF16); ktok_s = A("ktok_s", [256], BF16)
        sTm_s = A("sTm_s", [NS], BF16)
        qpad = [A(f"qpad{i}", [2, NS], BF16) for i in range(2)]
        kpad = [A(f"kpad{i}", [256], BF16) for i in range(2)]
        S0 = [A(f"S0_{i}", [2, 512], F32) for i in range(2)]
        S0b = A("S0b", [2, 512], BF16)
        for i in range(2):
            self.memset(qpad[i], 0.0, [f"qpad{i}"])
        HT, X, ident = self.HT, self.X, self.ident
        state, ss_out, sp_out = self.din["state"], self.dout["ss_out"], self.dout["sp_out"]
        wbi, wbk = self.bank()
        self.ps_pinned.add(wbi)

        def keep_warm(k):
            for _ in range(k):
                self.P.op("pe", lambda e: e.matmul(self.psf(wbi)[:, 0:128], lhsT=ident, rhs=ident, start=True, stop=True),
                          ["ident"], wbk, dur=0.1)
        gcnt = [0]
        pend_nb = []

        def rope(ps_i, keys, n, tok0, out, outkey, scale):
            x1 = self.psf(ps_i)[:, 0:n]; x2 = self.psf(ps_i + 1)[:, 0:n]
            cs = rcos[:, tok0:tok0 + n]; sn = rsin[:, tok0:tok0 + n]
            a = ta[:, 0:n]; b_ = tb[:, 0:n]
            self.stt(a, x1, scale, cs, ALU.mult, ALU.mult, [keys[0], "rcos"], ["ta"])
            self.stt(b_, x2, scale, sn, ALU.mult, ALU.mult, [keys[1], "rsin"], ["tb"])
            self.tt(out[:, 0, 0:n], a, b_, ALU.subtract, ["ta", "tb"], [outkey])
            self.stt(a, x2, scale, cs, ALU.mult, ALU.mult, [keys[1], "rcos"], ["ta"])
            self.stt(b_, x1, scale, sn, ALU.mult, ALU.mult, [keys[0], "rsin"], ["tb"])
            self.tt(out[:, 1, 0:n], a, b_, ALU.add, ["ta", "tb"], [outkey])

        def proj_fm(wname, n, tok0, htkeys):
            bi, bk = self.bank(2)
            w = self.W[wname]
            for oc in range(2):
                for dc in range(8):
                    self.mm(self.psf(bi + oc)[:, 0:n], w[:, dc, oc * 128:(oc + 1) * 128], HT[:, dc, tok0:tok0 + n],
                            dc == 0, dc == 7, [self.wkey[wname]] + htkeys, [bk[oc]])
            return bi, bk

        def proj_tm(wname, n, tok0, htkeys):
            bi, bk = self.bank(1)
            w = self.W[wname]
            for dc in range(8):
                self.mm(self.psf(bi)[0:n, :], HT[:, dc, tok0:tok0 + n], w[:, dc, :], dc == 0, dc == 7,
                        [self.wkey[wname]] + htkeys, bk)
            return bi, bk

        def gn_elem(n, o_bank, o_keys, sgbuf, sgkey):
            i = gcnt[0] % 2
            gcnt[0] += 1
            r_ = rst[:, i, :]; rk = ("rst", i)
            self.act(gated[0:n, :], self.psf(o_bank)[0:n, :], AF.Square, o_keys, ["gated", rk], accum_out=r_[0:n, 0:1])
            self.act(r_[0:n, 1:2], r_[0:n, 0:1], AF.Sqrt, [rk, "epsb"], [rk], scale=1.0 / RDV, bias=self.epsb[0:n, 0:1])
            self.recip(r_[0:n, 2:3], r_[0:n, 1:2], [rk], [rk])
            self.stt(gated[0:n, :], self.psf(o_bank)[0:n, :], r_[0:n, 2:3], sgbuf[0:n, :], ALU.mult, ALU.mult,
                     o_keys + [rk, sgkey], ["gated"])

        def gn_tr(n):
            ti, tk = self.bank()
            pb = self.psb(ti).rearrange("p (a b) -> p a b", a=8)
            for ec in range(4):
                self.tr(pb[:, ec, 0:n], gated[0:n, ec * 128:(ec + 1) * 128], ident[0:n, 0:n], ["gated", "ident"], tk)
            self.act(oTg[:, :, 0:n], pb[:, 0:4, 0:n], AF.Copy, tk, ["oTg"])

        def gn_out(h, n, xt):
            yi, yk = self.bank(2)
            wo = self.W[f"ro{h}"]
            for hf in range(2):
                for ec in range(4):
                    self.mm(self.psf(yi + hf)[0:n, :], oTg[:, ec, 0:n], wo[:, ec, hf * 512:(hf + 1) * 512], ec == 0, ec == 3,
                            ["oTg", self.wkey[f"ro{h}"]], [yk[hf]])
            xv = X[0:n, xt, :]
            self.tt(xv, xv, self.psf(yi, 2)[0:n, :], ALU.add, [("X", xt)] + yk, [("X", xt)])
            if h == RH - 1 and "nonext" not in self.dbg:
                if n == 128:
                    next_norm(xt, "a")
                    pend_nb.append(xt)
                else:
                    next_norm(xt)

        for h in range(RH):
            P.dma("sp", decT, c["c_decT"][:, h, :], slot="c2", writes=["decT"])
            P.dma("sp", qrow, c["c_qrow"][:, h, :], slot="c3", writes=["qrow"])
            P.dma("sp", sdec[0:NS], c["c_sdec"][:, h, :], slot="c5", writes=["sdec"])
            if h == 0:
                self.norm_tile(0, NT)
                for t in range(NT if "normup" in self.dbg else 4):
                    self.norm_tile(0, t)
            st_keys = [("HT", NT)]
            bi, bk = proj_fm(f"rq{h}", NS, SEQ, st_keys)
            rope(bi, bk, NS, SEQ, qr_s, "qr_s", 1.0)
            bi, bk = proj_fm(f"rk{h}", NS, SEQ, st_keys)
            rope(bi, bk, NS, SEQ, kr_s, "kr_s", 1.0 / 16.0)
            self.tt(qd_s, qr_s, qrow[:, 128:128 + NS].unsqueeze(1).broadcast_to([128, 2, NS]), ALU.mult, ["qr_s", "qrow"], ["qd_s"])
            bi, bk = proj_tm(f"rv{h}", NS, SEQ, st_keys)
            self.act(v_s[0:NS, :], self.psf(bi)[0:NS, :], AF.Copy, bk, ["v_s"])
            bi, bk = proj_tm(f"rg{h}", NS, SEQ, st_keys)
            self.act(sg_s[0:NS, :], self.psf(bi)[0:NS, :], AF.Silu, bk, ["sg_s"])
            ti, tk = self.bank()
            pb = self.psb(ti).rearrange("p (a b) -> p a b", a=8)
            for hf in range(2):
                self.tr(pb[0:NS, hf, :], kr_s[:, hf, :], ident, ["kr_s", "ident"], tk)
            self.act(ktok_s[0:NS, :], pb[0:NS, 0:2, :].rearrange("p a b -> p (a b)"), AF.Copy, tk + ["kcol"], ["ktok_s"],
                     scale=kcol[0:NS, RH + h:RH + h + 1])
            si, sk_ = self.bank()
            for hf in range(2):
                self.mm(self.psf(si)[0:NS, 0:NS], kr_s[:, hf, :], qr_s[:, hf, :], hf == 0, hf == 1, ["kr_s", "qr_s"], sk_)
            self.tt(sTm_s[0:NS, :], self.psf(si)[0:NS, 0:NS], sdec[0:NS, :], ALU.mult, sk_ + ["sdec"], ["sTm_s"])
            osi, osk = self.bank()
            self.ps_pinned.add(osi)
            self.mm(self.psf(osi)[0:NS, :], sTm_s[0:NS, :], v_s[0:NS, :], True, False, ["sTm_s", "v_s"], osk)
            for b in range(2):
                P.dma("sp", S0[b % 2], state[b, h].rearrange("(c p) e -> p c e", p=128), slot=("s0", b % 2),
                      writes=[f"S0_{b % 2}"])
            self.memset(Sf, 0.0, ["Sf"])
            self.memset(Sb, 0.0, ["Sb"])
            prev = None
            for s in range(4):
                tok0 = s * 512
                htk = [("HT", 4 * s + i) for i in range(4)]
                qb = qr[s % 2]; kb = kr[s % 2]; qbk = f"qr{s % 2}"; kbk = f"kr{s % 2}"
                bi, bk = proj_fm(f"rq{h}", 512, tok0, htk)
                rope(bi, bk, 512, tok0, qb, qbk, 1.0)
                bi, bk = proj_fm(f"rk{h}", 512, tok0, htk)
                rope(bi, bk, 512, tok0, kb, kbk, 1.0 / 16.0)
                for cc in range(4):
                    bi, bk = proj_tm(f"rg{h}", 128, tok0 + cc * 128, [("HT", 4 * s + cc)])
                    self.act(sg4[:, cc, :], self.psf(bi), AF.Silu, bk, [("sg4", cc)])
                for cc in range(4):
                    cidx = s * 4 + cc
                    c0 = cc * 128
                    t0 = cidx * 128
                    hk1 = [("HT", cidx)]
                    i2 = cidx % 2
                    qdb = qd[i2]; qdk = f"qd{i2}"
                    stb = sTm[i2]; stk = f"sTm{i2}"
                    ktb = ktok[i2]; ktk = f"ktok{i2}"
                    vcb = vc[i2]; vck = f"vc{i2}"
                    sgb = sg4[:, cc, :]; sgk = ("sg4", cc)
                    b = cidx
                    r = b % 2
                    s0k = f"S0_{r}"; qpk = f"qpad{r}"; kpk = f"kpad{r}"
                    ti, tk = self.bank()
                    pb = self.psb(ti).rearrange("p (a b) -> p a b", a=8)
                    for hf in range(2):
                        self.tr(pb[:, hf, :], kb[:, hf, c0:c0 + 128], ident, [kbk, "ident"], tk)
                    self.act(ktb, pb[:, 0:2, :].rearrange("p a b -> p (a b)"), AF.Copy, tk + ["kcol"], [ktk], scale=kcol[:, h:h + 1])
                    self.act(S0b.rearrange("p a b -> p (a b)"), S0[r].rearrange("p a b -> p (a b)"), AF.Copy, [s0k], ["S0b"])
                    self.cp(qpad[r][:, :, 4 * b:4 * b + 4], qd_s[:, :, 4 * b:4 * b + 4], ["qd_s"], [qpk])
                    self.ts(kpad[r][0:NS, :], ktok_s[0:NS, :], rowmask[0:NS, b:b + 1], None, ALU.mult, None, ["ktok_s", "rowmask"], [kpk])
                    bi, bk = proj_tm(f"rv{h}", 128, t0, hk1)
                    self.act(vcb, self.psf(bi), AF.Copy, bk, [vck])
                    ui, uk = self.bank(2)
                    for hf in range(2):
                        self.mm(self.psf(ui + hf), ktb[:, hf * 128:(hf + 1) * 128], vcb, True, True, [ktk, vck], [uk[hf]])
                    Sfv = Sf.rearrange("p a b -> p (a b)")
                    self.stt(Sfv, Sfv, g128[h], self.psf(ui, 2), ALU.mult, ALU.add, ["Sf"] + uk, ["Sf"])
                    for hf in range(2):
                        self.mm(self.psf(osi)[0:NS, :], qpad[r][:, hf, :], S0b[:, hf, :], False, (b == NB - 1 and hf == 1),
                                [qpk, "S0b"], osk)
                    self.memset(qpad[r][:, :, 4 * b:4 * b + 4], 0.0, [qpk])
                    ui, uk = self.bank(2)
                    for hf in range(2):
                        self.mm(self.psf(ui + hf), kpad[r][0:NS, hf * 128:(hf + 1) * 128], v_s[0:NS, :], True, True, [kpk, "v_s"], [uk[hf]])
                    S0v = S0[r].rearrange("p a b -> p (a b)")
                    self.stt(S0v, S0v, g4[h], self.psf(ui, 2), ALU.mult, ALU.add, [s0k] + uk, [s0k])
                    P.dma("sp", ss_out[b, h].rearrange("(c p) e -> p c e", p=128), S0[r], slot=("sso", r), reads=[s0k])
                    if b + 2 < NB:
                        P.dma("sp", S0[r], state[b + 2, h].rearrange("(c p) e -> p c e", p=128), slot=("s0", r), writes=[s0k])
                    self.tt(qdb, qb[:, :, c0:c0 + 128], qrow[:, 0:128].unsqueeze(1).broadcast_to([128, 2, 128]), ALU.mult,
                            [qbk, "qrow"], [qdk])
                    si, sk_ = self.bank()
                    for hf in range(2):
                        self.mm(self.psf(si)[:, 0:128], kb[:, hf, c0:c0 + 128], qb[:, hf, c0:c0 + 128], hf == 0, hf == 1, [kbk, qbk], sk_)
                    self.tt(stb, self.psf(si)[:, 0:128], decT, ALU.mult, sk_ + ["decT"], [stk])
                    while len(pend_nb) > 1 or (pend_nb and pend_nb[0] < cidx - 1):
                        next_norm(pend_nb.pop(0), "b")
                    if prev is not None:
                        gn_tr(128)
                    keep_warm(WARM_MM)
                    oi, ok_ = self.bank()
                    self.mm(self.psf(oi), stb, vcb, True, False, [stk, vck], ok_)
                    for hf in range(2):
                        self.mm(self.psf(oi), qdb[:, hf, :], Sb[:, hf, :], False, hf == 1, [qdk, "Sb"], ok_)
                    if cidx < NT - 1:
                        self.act(Sb.rearrange("p a b -> p (a b)"), Sfv, AF.Copy, ["Sf"], ["Sb"])
                    gn_elem(128, oi, ok_, sgb, sgk)
                    if prev is not None:
                        gn_out(h, 128, prev)
                    prev = cidx
                    if "nopipe" in self.dbg:
                        gn_tr(128)
                        gn_out(h, 128, prev)
                        prev = None
                    if h == 0 and s < 3 and "normup" not in self.dbg:
                        self.norm_tile(0, 4 * (s + 1) + cc)
            if prev is not None:
                gn_tr(128)
                gn_out(h, 128, prev)
            while pend_nb:
                next_norm(pend_nb.pop(0), "b")
            P.dma("sp", sp_out[h].rearrange("(c p) e -> p c e", p=128), Sf, slot="spo", reads=["Sf"])
            self.ps_pinned.discard(osi)
            gn_elem(NS, osi, osk, sg_s, "sg_s")
            gn_tr(NS)
            gn_out(h, NS, NT)
        self.ps_pinned.discard(wbi)
        self.free("rcos", "rsin", "decT", "qrow", "kcol", "sdec", "rowmask", "qr0", "qr1", "kr0", "kr1", "qd0", "qd1", "ta", "tb",
                  "vc0", "vc1", "sg4", "sTm0", "sTm1", "ktok0", "ktok1", "gated", "oTg", "Sf", "Sb", "rst", "qr_s", "kr_s", "qd_s",
                  "v_s", "sg_s", "ktok_s", "sTm_s", "qpad0", "qpad1", "kpad0", "kpad1", "S0_0", "S0_1", "S0b")

    def ffn(self, l, last, next_norm=None):
        P = self.P
        A = self.alloc
        t = f"f{l}"
        sgl = [A(f"sgl{t}{i}", [512], BF16) for i in range(2)]
        aT = [A(f"aT{t}{i}", [4, 512], BF16) for i in range(2)]
        HT, X = self.HT, self.X
        tiles = [(s * 512, 512, [("HT", 4 * s + i) for i in range(4)]) for s in range(4)] + [(SEQ, NS, [("HT", NT)])]
        ng = len(FFN_GROUPS)
        uidx = 0
        for gi, G in enumerate(FFN_GROUPS):
            wg, wu, wo = self.W[f"fg{l}_{gi}"], self.W[f"fu{l}_{gi}"], self.W[f"fo{l}_{gi}"]
            kg, ku, ko = self.wkey[f"fg{l}_{gi}"], self.wkey[f"fu{l}_{gi}"], self.wkey[f"fo{l}_{gi}"]
            for si, (tok0, n, htk) in enumerate(tiles):
                ab = aT[si % 2]; abk = f"aT{t}{si % 2}"
                for u in range(G):
                    sb = sgl[uidx % 2]; sbk = f"sgl{t}{uidx % 2}"
                    uidx += 1
                    gi_, gk = self.bank()
                    for dc in range(8):
                        self.mm(self.psf(gi_)[:, 0:n], wg[:, dc, u * 128:(u + 1) * 128], HT[:, dc, tok0:tok0 + n], dc == 0, dc == 7,
                                [kg] + htk, gk)
                    ui_, uk = self.bank()
                    for dc in range(8):
                        self.mm(self.psf(ui_)[:, 0:n], wu[:, dc, u * 128:(u + 1) * 128], HT[:, dc, tok0:tok0 + n], dc == 0, dc == 7,
                                [ku] + htk, uk)
                    self.act(sb[:, 0:n], self.psf(gi_)[:, 0:n], AF.Silu, gk, [sbk])
                    self.tt(ab[:, u, 0:n], sb[:, 0:n], self.psf(ui_)[:, 0:n], ALU.mult, [sbk] + uk, [(abk, u)])
                pend_norm = []
                for tt_ in range((n + 127) // 128):
                    m = min(128, n - tt_ * 128)
                    xt = (tok0 // 128 + tt_) if tok0 < SEQ else NT
                    yi, yk = self.bank(2)
                    for hf in range(2):
                        for u in range(G):
                            self.mm(self.psf(yi + hf)[0:m, :], ab[:, u, tt_ * 128:tt_ * 128 + m], wo[:, u, hf * 512:(hf + 1) * 512],
                                    u == 0, u == G - 1, [(abk, u), ko], [yk[hf]])
                    xv = X[0:m, xt, :]
                    self.tt(xv, xv, self.psf(yi, 2)[0:m, :], ALU.add, [("X", xt)] + yk, [("X", xt)])
                    if next_norm is not None and gi == ng - 1:
                        pend_norm.append(xt)
                    if last and gi == ng - 1:
                        if xt < NT:
                            P.dma("sp", self.dout["yp"][xt * 128:(xt + 1) * 128, :], xv, slot="yo", reads=[("X", xt)], new_gen=False)
                        else:
                            P.dma("sp", self.dout["ys"][:, :], xv, slot="yo", reads=[("X", xt)], new_gen=False)
                for xt in pend_norm:
                    next_norm(xt)
        self.free(*[f"sgl{t}{i}" for i in range(2)], *[f"aT{t}{i}" for i in range(2)])

    def swa_consts(self):
        P = self.P
        A = self.alloc
        c = self.din
        scos = A("scos", [NT + 1, 32], F32); ssin = A("ssin", [NT + 1, 32], F32)
        gqk = A("gqk", [20, 64], F32)
        esink = A("esink", [16], F32); esinkp = A("esinkp", [16], F32)
        ones = A("ones", [128], BF16)
        P.dma("sp", scos, c["c_scos"][:, :, :], slot="c0", writes=["scos"])
        P.dma("sp", ssin, c["c_ssin"][:, :, :], slot="c1", writes=["ssin"])
        for hh in range(16):
            P.dma("sp", gqk[:, hh, :], c["swa_q_norm"][0:1, :].partition_broadcast(128), slot="c5", writes=[("gqk", hh)], new_gen=False)
        for hh in range(4):
            P.dma("sp", gqk[:, 16 + hh, :], c["swa_k_norm"][0:1, :].partition_broadcast(128), slot="c5", writes=[("gqk", 16 + hh)], new_gen=False)
        P.dma("sp", esink, c["swa_sinks"][0:1, :].partition_broadcast(128), slot="c6", writes=["esink"])
        for hk_ in range(4):
            for hf in range(2):
                for g_ in range(2):
                    hd = 4 * hk_ + 2 * g_ + hf
                    col = hk_ * 4 + hf * 2 + g_
                    P.dma("sp", esinkp[:, col:col + 1], c["swa_sinks"][0:1, hd:hd + 1].partition_broadcast(128), slot="c7",
                          writes=[("esinkp", col)], new_gen=False)
        self.memset(ones, 1.0, ["ones"])
        self.swa_c = (scos, ssin, gqk, esink, esinkp, ones)

    def swa(self, next_norm):
        P = self.P
        A = self.alloc
        c = self.din
        HT, X, ident = self.HT, self.X, self.ident
        scos, ssin, gqk, esink, esinkp, ones = self.swa_c
        band, cmask, nmask = self.band, self.cmask, self.nmask
        gqk_keys = [("gqk", i) for i in range(20)]
        esp_keys = [("esinkp", i) for i in range(16)]
        self.act(esink, esink, AF.Exp, ["esink"], ["esink"])
        self.act(esinkp, esinkp, AF.Exp, esp_keys, ["esinkp"])
        qk = A("qk", [20, 64], F32); sq = A("sq", [20, 64], BF16); qkr = A("qkr", [20, 64], F32)
        ta = A("sta", [20, 32], F32); tb = A("stb", [20, 32], F32)
        sst = A("sst", [3, 20], F32)
        q_r = A("q_r", [16, 64], BF16); kdup = A("kdup", [4, 2, 64], BF16)
        vf = A("vf", [256], F32)
        vaug = [A(f"vaug{i}", [4, 65], BF16) for i in range(3)]
        qTs = [A(f"qT{i}", [8, 128], BF16) for i in range(2)]
        kTd = [A(f"kTd{i}", [4, 128], BF16) for i in range(3)]
        pT = [A(f"pT{i}", [2, 512], BF16) for i in range(2)]
        o_n = A("o_n", [16, 64], BF16); oT = A("oT", [8, 128], BF16)
        dn = A("dn", [2, 4], F32)
        for i in range(3):
            self.memset(vaug[i][:, :, 64:65], 1.0, [f"vaug{i}"])
        wsi = [self.W[f"si{j}"] for j in range(3)]
        ksi = [self.wkey[f"si{j}"] for j in range(3)]
        wso, kso = self.W["so"], self.wkey["so"]

        def front_proj(n, tok0, htk):
            bi, bk = self.bank(3)
            for j in range(3):
                for dc in range(8):
                    self.mm(self.psf(bi + j)[0:n, :], HT[:, dc, tok0:tok0 + n], wsi[j][:, dc, :], dc == 0, dc == 7, [ksi[j]] + htk, [bk[j]])
            return bi, bk

        def front_parts(n, tbl, bi, bk, vdst_ops):
            ps = self.psf(bi, 3)
            qkf = qk.rearrange("p a b -> p (a b)")
            sqf = sq.rearrange("p a b -> p (a b)")
            x1 = qk[0:n, :, 0:32]; x2 = qk[0:n, :, 32:64]
            cs = scos[0:n, tbl, :].unsqueeze(1).broadcast_to([n, 20, 32]); sn = ssin[0:n, tbl, :].unsqueeze(1).broadcast_to([n, 20, 32])

            def p0():
                self.act(qkf[0:n, :], ps[0:n, 0:1280], AF.Copy, bk, ["qk"])
                vdst_ops(ps, bk)
                self.act(sqf[0:n, :], qkf[0:n, :], AF.Square, ["qk"], ["sq"])

            def p1():
                self.reduce(sst[0:n, 0, :], sq[0:n], ALU.add, ["sq"], [("sst", 0)])
                self.act(sst[0:n, 1, :], sst[0:n, 0, :], AF.Ln, [("sst", 0), "epsb"], [("sst", 1)], scale=1.0 / HD, bias=self.epsb[0:n, 0:1])
                self.act(sst[0:n, 2, :], sst[0:n, 1, :], AF.Exp, [("sst", 1)], [("sst", 2)], scale=-0.5)
                self.tt(qk[0:n], qk[0:n], sst[0:n, 2, :].unsqueeze(2).broadcast_to([n, 20, 64]), ALU.mult, ["qk", ("sst", 2)], ["qk"])
                self.tt(qk[0:n], qk[0:n], gqk[0:n], ALU.mult, ["qk"] + gqk_keys, ["qk"])

            def p2():
                self.tt(ta[0:n], x1, cs, ALU.mult, ["qk", "scos"], ["sta"])
                self.tt(tb[0:n], x2, sn, ALU.mult, ["qk", "ssin"], ["stb"])
                self.tt(qkr[0:n, :, 0:32], ta[0:n], tb[0:n], ALU.subtract, ["sta", "stb"], ["qkr"])

            def p3():
                self.tt(ta[0:n], x2, cs, ALU.mult, ["qk", "scos"], ["sta"])
                self.tt(tb[0:n], x1, sn, ALU.mult, ["qk", "ssin"], ["stb"])
                self.tt(qkr[0:n, :, 32:64], ta[0:n], tb[0:n], ALU.add, ["sta", "stb"], ["qkr"])
                self.act(q_r[0:n], qkr[0:n, 0:16, :], AF.Copy, ["qkr"], ["q_r"])
                self.act(kdup[0:n], qkr[0:n, 16:20, :].unsqueeze(2).broadcast_to([n, 4, 2, 64]), AF.Copy, ["qkr"], ["kdup"])
            return [p0, p1, p2, p3]

        def qkv_front(n, tok0, htk, tbl, vdst_ops):
            bi, bk = front_proj(n, tok0, htk)
            for p in front_parts(n, tbl, bi, bk, vdst_ops):
                p()

        nblk = self.swa_blocks

        def mk_vdst(n_):
            vb = vaug[n_ % 3]; vbk = f"vaug{n_ % 3}"

            def vdst(ps, bk):
                self.act(vb[:, :, 0:64], ps[:, 1280:1536].rearrange("p (a b) -> p a b", a=4), AF.Copy, bk, [vbk])
                if n_ == NT - 1:
                    self.act(vf, ps[:, 1280:1536], AF.Copy, bk, ["vf"])
                    P.dma("sp", self.dout["vp"][:, :], vf, slot="vpo", reads=["vf"])
            return vdst

        def front_tr(n_):
            qT = qTs[n_ % 2]; qTk = f"qT{n_ % 2}"
            ktb = kTd[n_ % 3]; ktk = f"kTd{n_ % 3}"
            if n_ == NT - 1:
                P.dma("sp", self.dout["kp"][:, :], qkr[:, 16:20, :].rearrange("p a b -> p (a b)"), slot="kpo", reads=["qkr"])
            ti, tk = self.bank()
            pb = self.psb(ti).rearrange("p (a b) -> p a b", a=8)
            for cc in range(8):
                self.tr(pb[:, cc, :], q_r[:, 2 * cc:2 * cc + 2, :].rearrange("p a b -> p (a b)"), ident, ["q_r", "ident"], tk)
            self.act(qT, pb, AF.Copy, tk, [qTk])
            ti, tk = self.bank()
            pb = self.psb(ti).rearrange("p (a b) -> p a b", a=8)
            for hk_ in range(4):
                self.tr(pb[:, hk_, :], kdup[:, hk_, :, :].rearrange("p a b -> p (a b)"), ident, ["kdup", "ident"], tk)
            self.act(ktb, pb[:, 0:4, :], AF.Copy, tk, [ktk])

        def scores(n_, hk_):
            qT = qTs[n_ % 2]; qTk = f"qT{n_ % 2}"
            kbs = ([n_ - 1] if n_ > 0 else []) + [n_]
            nkb = len(kbs)
            pt = pT[hk_ % 2]; ptk = f"pT{hk_ % 2}"
            si, sk_ = self.bank(2)
            w_ = nkb * 256
            for kbi, kb_ in enumerate(kbs):
                for hf in range(2):
                    self.mm(self.psf(si + hf)[:, kbi * 256:(kbi + 1) * 256],
                            kTd[kb_ % 3][hf * 64:(hf + 1) * 64, hk_, :], qT[hf * 64:(hf + 1) * 64, 2 * hk_:2 * hk_ + 2, :],
                            True, True, [f"kTd{kb_ % 3}", qTk], [sk_[hf]])
            m0 = 0 if n_ > 0 else 1
            for hf in range(2):
                self.act(pt[:, hf, 0:w_], self.psf(si + hf)[:, 0:w_], AF.Exp, [sk_[hf]], [ptk], scale=HD ** -0.5)
                pv4 = pt[:, hf, 0:w_].rearrange("p (k g q) -> p k g q", g=2, q=128)
                self.tt(pv4, pv4, band[:, m0:m0 + nkb, :].unsqueeze(2).broadcast_to([128, nkb, 2, 128]), ALU.mult, [ptk, "band"], [ptk])

        def pvnorm(n_, hk_):
            kbs = ([n_ - 1] if n_ > 0 else []) + [n_]
            nkb = len(kbs)
            pt = pT[hk_ % 2]; ptk = f"pT{hk_ % 2}"
            pvi, pvk = self.bank()
            pv = self.psf(pvi)[:, 0:260].rearrange("p (a b) -> p a b", a=4)
            for j in range(4):
                hf, g_ = j % 2, j // 2
                for kbi, kb_ in enumerate(kbs):
                    self.mm(pv[:, j, :], pt[:, hf, kbi * 256 + g_ * 128:kbi * 256 + g_ * 128 + 128], vaug[kb_ % 3][:, hk_, :],
                            kbi == 0, kbi == nkb - 1, [ptk, f"vaug{kb_ % 3}"], pvk)
            self.tt(dn[:, 0, :], pv[:, :, 64], esink[:, 4 * hk_:4 * hk_ + 4], ALU.add, pvk + ["esink"], [("dn", 0)])
            self.recip(dn[:, 1, :], dn[:, 0, :], [("dn", 0)], [("dn", 1)])
            self.tt(o_n[:, 4 * hk_:4 * hk_ + 4, :], pv[:, :, 0:64], dn[:, 1, :].unsqueeze(2).broadcast_to([128, 4, 64]), ALU.mult,
                    pvk + [("dn", 1)], ["o_n"])

        def o_tr(n_):
            ti, tk = self.bank()
            pb = self.psb(ti).rearrange("p (a b) -> p a b", a=8)
            for cc in range(8):
                self.tr(pb[:, cc, :], o_n[:, 2 * cc:2 * cc + 2, :].rearrange("p a b -> p (a b)"), ident, ["o_n", "ident"], tk)
            self.act(oT, pb, AF.Copy, tk, ["oT"])

        def o_mm(n_):
            yi, yk = self.bank(2)
            for hf in range(2):
                for cc in range(8):
                    self.mm(self.psf(yi + hf), oT[:, cc, :], wso[:, cc, hf * 512:(hf + 1) * 512], cc == 0, cc == 7, ["oT", kso], [yk[hf]])
            xv = X[:, n_, :]
            self.tt(xv, xv, self.psf(yi, 2), ALU.add, [("X", n_)] + yk, [("X", n_)])
            next_norm(n_, "a")

        if nblk > 0:
            bi, bk = front_proj(128, 0, [("HT", 0)])
            for p in front_parts(128, 0, bi, bk, mk_vdst(0)):
                p()
            front_tr(0)
        for n_ in range(nblk + 1):
            cur = n_ < nblk
            nxt = n_ + 1 < nblk
            prv = n_ - 1 if n_ > 0 else None
            parts = None
            if cur:
                scores(n_, 0)
            if prv is not None:
                o_tr(prv)
            for hk_ in range(4):
                if cur:
                    if hk_ + 1 < 4:
                        scores(n_, hk_ + 1)
                    pvnorm(n_, hk_)
                if hk_ == 0 and nxt:
                    bi, bk = front_proj(128, (n_ + 1) * 128, [("HT", n_ + 1)])
                    parts = front_parts(128, n_ + 1, bi, bk, mk_vdst(n_ + 1))
                    parts[0]()
                if hk_ == 1 and prv is not None:
                    o_mm(prv)
                if hk_ == 1 and nxt:
                    parts[1]()
                if hk_ == 2 and nxt:
                    parts[2]()
                if hk_ == 2 and prv is not None:
                    next_norm(prv, "b")
                if hk_ == 3 and nxt:
                    parts[3]()
            if nxt:
                front_tr(n_ + 1)

        if "nosample" in self.dbg:
            self.free("scos", "ssin", "gqk", "esink", "esinkp", "ones", "qk", "sq", "qkr", "sta", "stb", "sst", "q_r", "kdup",
                      "vf", "vaug0", "vaug1", "vaug2", "qT0", "qT1", "kTd0", "kTd1", "kTd2", "pT0", "pT1", "o_n", "oT", "dn")
            return
        ck, cv = c["ck"], c["cv"]
        ks_out, vs_out = self.dout["ks"], self.dout["vs"]
        P.dma("sp", ks_out[:, 0:124, :], ck[:, 4:128, :], slot="kso")
        P.dma("sp", vs_out[:, 0:124, :], cv[:, 4:128, :], slot="vso")
        self.free("vf", "vaug0", "vaug1", "vaug2", "qT0", "qT1", "kTd0", "kTd1", "kTd2", "pT0", "pT1", "o_n", "oT", "dn")
        vfs = A("vfs", [256], F32); vdn = A("vdn", [4, 2, 64], BF16)
        qT_s = A("qT_s", [8, NS], BF16); kTd_s = A("kTd_s", [4, NS], BF16)
        pn = A("pn", [16, NS], BF16)
        kc = [A(f"kc{i}", [256], F32) for i in range(2)]; vcc = [A(f"vcc{i}", [256], F32) for i in range(2)]
        kcd = A("kcd", [4, 2, 64], BF16); vcd = [A(f"vcd{i}", [4, 2, 64], BF16) for i in range(2)]
        kTc = A("kTc", [4, 128], BF16); pc = [A(f"pc{i}", [16, TS], BF16) for i in range(2)]
        dsb = A("dsb", [16, NS], F32); o_sn = A("o_sn", [16, NS], BF16)

        def vdst_s(ps, bk):
            self.act(vfs[0:NS, :], ps[0:NS, 1280:1536], AF.Copy, bk, ["vfs"])
            self.act(vdn[0:NS], ps[0:NS, 1280:1536].rearrange("p (a b) -> p a b", a=4).unsqueeze(2).broadcast_to([NS, 4, 2, 64]), AF.Copy,
                     bk, ["vdn"])
        qkv_front(NS, SEQ, [("HT", NT)], NT, vdst_s)
        for b in range(NB):
            P.dma("sp", ks_out[b, 124:128, :], qkr[4 * b:4 * b + 4, 16:20, :].rearrange("p a b -> p (a b)"), slot="kso2", reads=["qkr"],
                  new_gen=False)
            P.dma("sp", vs_out[b, 124:128, :], vfs[4 * b:4 * b + 4, :], slot="vso2", reads=["vfs"], new_gen=False)
        ti, tk = self.bank()
        pb = self.psb(ti).rearrange("p (a b) -> p a b", a=8)
        for cc in range(8):
            self.tr(pb[:, cc, 0:NS], q_r[0:NS, 2 * cc:2 * cc + 2, :].rearrange("p a b -> p (a b)"), ident[0:NS, 0:NS], ["q_r", "ident"], tk)
        self.act(qT_s, pb[:, :, 0:NS], AF.Copy, tk, ["qT_s"])
        ti, tk = self.bank()
        pb = self.psb(ti).rearrange("p (a b) -> p a b", a=8)
        for hk_ in range(4):
            self.tr(pb[:, hk_, 0:NS], kdup[0:NS, hk_, :, :].rearrange("p a b -> p (a b)"), ident[0:NS, 0:NS], ["kdup", "ident"], tk)
        self.act(kTd_s, pb[:, 0:4, 0:NS], AF.Copy, tk, ["kTd_s"])
        si, sk_ = self.bank(2)
        for hf in range(2):
            for hk_ in range(4):
                self.mm(self.psf(si + hf)[0:NS, hk_ * 128:(hk_ + 1) * 128], kTd_s[hf * 64:(hf + 1) * 64, hk_, :],
                        qT_s[hf * 64:(hf + 1) * 64, 2 * hk_:2 * hk_ + 2, :], True, True, ["kTd_s", "qT_s"], [sk_[hf]])
        for hk_ in range(4):
            pv_ = pn[0:NS, 4 * hk_:4 * hk_ + 4, :].rearrange("p a (b x t) -> p (a b) x t", x=4, t=TS)
            for hf in range(2):
                self.act(pv_[:, :, 2 * hf:2 * hf + 2, :],
                         self.psf(si + hf)[0:NS, hk_ * 128:(hk_ + 1) * 128].rearrange("p (g b t) -> p b g t", g=2, t=TS), AF.Exp,
                         [sk_[hf]], ["pn"], scale=HD ** -0.5)
            self.tt(pv_, pv_, nmask[0:NS, :].rearrange("p (b t) -> p b t", t=TS).unsqueeze(2).broadcast_to([NS, NB, 4, TS]), ALU.mult,
                    ["pn", "nmask"], ["pn"])
        oi, ok_ = self.bank(2)
        di, dk = self.bank(2)
        for j in range(2):
            self.ps_pinned.add(oi + j); self.ps_pinned.add(di + j)
        oacc = self.psf(oi, 2)
        dacc = self.psf(di, 2)
        pnf = pn.rearrange("p a b -> p (a b)")
        for hk_ in range(4):
            self.mm(oacc[:, hk_ * 256:(hk_ + 1) * 256], vdn[0:NS, hk_, :, :].rearrange("p a b -> p (a b)"), pnf[0:NS, hk_ * 256:(hk_ + 1) * 256],
                    hk_ % 2 == 0, False, ["vdn", "pn"], [ok_[hk_ // 2]])
        for j in range(2):
            self.mm(dacc[:, j * 512:(j + 1) * 512], ones[0:NS, :], pnf[0:NS, j * 512:(j + 1) * 512], True, False, ["ones", "pn"], [dk[j]])
        for b in range(2):
            P.dma("sp", kc[b % 2], ck[b, :, :], slot=("kc", b % 2), writes=[f"kc{b % 2}"])
            P.dma("sp", vcc[b % 2], cv[b, :, :], slot=("vc", b % 2), writes=[f"vcc{b % 2}"])
        for b in range(NB):
            r = b % 2
            lastb = (b == NB - 1)
            self.act(kcd, kc[r].rearrange("p (a b) -> p a b", a=4).unsqueeze(2).broadcast_to([128, 4, 2, 64]), AF.Copy, [f"kc{r}"], ["kcd"])
            self.act(vcd[r], vcc[r].rearrange("p (a b) -> p a b", a=4).unsqueeze(2).broadcast_to([128, 4, 2, 64]), AF.Copy, [f"vcc{r}"],
                     [f"vcd{r}"])
            if b + 2 < NB:
                P.dma("sp", kc[r], ck[b + 2, :, :], slot=("kc", r), writes=[f"kc{r}"])
                P.dma("sp", vcc[r], cv[b + 2, :, :], slot=("vc", r), writes=[f"vcc{r}"])
            ti, tk = self.bank()
            pb = self.psb(ti).rearrange("p (a b) -> p a b", a=8)
            for hk_ in range(4):
                self.tr(pb[:, hk_, :], kcd[:, hk_, :, :].rearrange("p a b -> p (a b)"), ident, ["kcd", "ident"], tk)
            self.act(kTc, pb[:, 0:4, :], AF.Copy, tk, ["kTc"])
            si, sk_ = self.bank(2)
            for hf in range(2):
                for hk_ in range(4):
                    self.mm(self.psf(si + hf)[:, hk_ * 8:(hk_ + 1) * 8].rearrange("p (a b) -> p a b", a=2), kTc[hf * 64:(hf + 1) * 64, hk_, :],
                            qT_s[hf * 64:(hf + 1) * 64, 2 * hk_:2 * hk_ + 2, 4 * b:4 * b + 4], True, True, ["kTc", "qT_s"], [sk_[hf]])
            pcb = pc[r]; pck = f"pc{r}"
            pc4 = pcb.rearrange("p (k h g) t -> p k h (g t)", k=4, h=2)
            for hf in range(2):
                self.act(pc4[:, :, hf, :], self.psf(si + hf)[:, 0:32].rearrange("p (k x) -> p k x", k=4), AF.Exp, [sk_[hf]], [pck],
                         scale=HD ** -0.5)
            self.tt(pcb, pcb, cmask[:, :].unsqueeze(1).broadcast_to([128, 16, TS]), ALU.mult, [pck, "cmask"], [pck])
            pcf = pcb.rearrange("p a b -> p (a b)")
            for hk_ in range(4):
                c0 = hk_ * 256 + b * 16
                self.mm(oacc[:, c0:c0 + 16], vcd[r][:, hk_, :, :].rearrange("p a b -> p (a b)"), pcf[:, hk_ * 16:(hk_ + 1) * 16],
                        False, lastb and hk_ % 2 == 1, [f"vcd{r}", pck], [ok_[hk_ // 2]])
                self.mm(dacc[:, c0:c0 + 16], ones, pcf[:, hk_ * 16:(hk_ + 1) * 16], False, lastb and hk_ % 2 == 1, ["ones", pck], [dk[hk_ // 2]])
        for j in range(2):
            self.ps_pinned.discard(oi + j); self.ps_pinned.discard(di + j)
        dsf = dsb.rearrange("p a b -> p (a b)")
        for hk_ in range(4):
            sl = slice(hk_ * 256, (hk_ + 1) * 256)
            self.tt(dsf[:, sl].rearrange("p (b x t) -> p b x t", x=4, t=TS), dacc[:, sl].rearrange("p (b x t) -> p b x t", x=4, t=TS),
                    esinkp[:, 4 * hk_:4 * hk_ + 4].unsqueeze(1).unsqueeze(3).broadcast_to([128, NB, 4, TS]), ALU.add,
                    [dk[hk_ // 2], "esinkp"], ["dsb"])
        self.recip(dsb, dsb, ["dsb"], ["dsb"])
        for hk_ in range(4):
            sl = slice(hk_ * 256, (hk_ + 1) * 256)
            self.tt(o_sn[:, 4 * hk_:4 * hk_ + 4, :].rearrange("p x (b t) -> p b x t", t=TS),
                    oacc[:, sl].rearrange("p (b x t) -> p b x t", x=4, t=TS), dsf[:, sl].rearrange("p (b x t) -> p b x t", x=4, t=TS),
                    ALU.mult, [ok_[hk_ // 2], "dsb"], ["o_sn"])
        xv = X[0:NS, NT, :]
        for hf2 in range(2):
            yi, yk = self.bank(2)
            for hf in range(2):
                idx = 0
                for hk_ in range(4):
                    for g_ in range(2):
                        cch = 2 * hk_ + g_
                        lh = o_sn[hf * 64:(hf + 1) * 64, hk_ * 4 + hf * 2 + g_, :]
                        self.mm(self.psf(yi + hf)[0:NS, :], lh, wso[hf * 64:(hf + 1) * 64, cch, hf2 * 512:(hf2 + 1) * 512],
                                idx == 0, idx == 7, ["o_sn", kso], [yk[hf]])
                        idx += 1
            xh = xv[:, hf2 * 512:(hf2 + 1) * 512]
            for hf in range(2):
                self.tt(xh, xh, self.psf(yi + hf)[0:NS, :], ALU.add, [("X", NT), yk[hf]], [("X", NT)])
        next_norm(NT)
        self.free("scos", "ssin", "gqk", "esink", "esinkp", "ones", "qk", "sq", "qkr", "sta", "stb", "sst",
                  "q_r", "kdup",
                  "vfs", "vdn", "qT_s", "kTd_s", "pn", "kc0", "kc1", "vcc0", "vcc1", "kcd", "vcd0", "vcd1", "kTc", "pc0", "pc1",
                  "dsb", "o_sn")

    def build(self, consts):
        P = self.P
        inp, outp = self.inp, self.outp
        inp("xp", [SEQ, D]); inp("xs", [NS, D]); inp("state", [NB, RH, RDK, RDV]); inp("ck", [NB, 128, 256]); inp("cv", [NB, 128, 256])
        inp("norm_mix", [2, D]); inp("norm_ffn", [2, D])
        inp("w_ret_in", [D, 6144]); inp("w_ret_out", [2048, D]); inp("w_swa_in", [D, 1536]); inp("w_swa_out", [D, D])
        inp("swa_q_norm", [1, 64]); inp("swa_k_norm", [1, 64]); inp("swa_sinks", [1, 16])
        inp("w_ffn_in0", [D, 2 * DFF]); inp("w_ffn_in1", [D, 2 * DFF]); inp("w_ffn_out0", [DFF, D]); inp("w_ffn_out1", [DFF, D])
        for k, v in consts.items():
            inp(k, v.shape)
        outp("yp", [SEQ, D]); outp("ys", [NS, D]); outp("sp_out", [RH, RDK, RDV]); outp("ss_out", [NB, RH, RDK, RDV])
        outp("kp", [128, 256]); outp("vp", [128, 256]); outp("ks", [NB, 128, 256]); outp("vs", [NB, 128, 256])
        st = self.stage
        if isinstance(st, (set, frozenset, tuple, list)):
            self.only = set(st)
        else:
            self.only = set(["ret", "ffn0", "swa", "ffn1"][:min(int(st), 4)])
        self.init_mem(R_F32)
        self.init_psum()
        self.X = self.alloc("Xbuf", [NT + 1, D], F32)
        self.HT = self.alloc("HTbuf", [8, TOKS], BF16)
        self.WA = self.alloc("WAbuf", [W_ELEMS], BF16)
        self.ident = self.alloc("ident", [128], BF16)
        self.epsb = self.alloc("epsb", [1], F32)
        self.memset(self.epsb, EPS, ["epsb"])
        P.dma("pool", self.ident, self.din["c_ident"][:, :], slot="cid", writes=["ident"])
        self.band = self.alloc("band", [2, 128], BF16)
        self.cmask = self.alloc("cmask", [TS], BF16)
        self.nmask = self.alloc("nmask", [NS], BF16)
        self.rcos = self.alloc("rcos", [TOKS], BF16)
        self.rsin = self.alloc("rsin", [TOKS], BF16)
        cdin = self.din
        P.dma("pool", self.rcos, cdin["c_rcos"][:, :], slot="pc0", writes=["rcos"])
        P.dma("pool", self.rsin, cdin["c_rsin"][:, :], slot="pc1", writes=["rsin"])
        P.dma("pool", self.band, cdin["c_band"][:, :, :], slot="pc2", writes=["band"])
        P.dma("pool", self.cmask, cdin["c_cmask"][:, :], slot="pc3", writes=["cmask"])
        P.dma("pool", self.nmask[0:NS], cdin["c_nmask"][:, :], slot="pc4", writes=["nmask"])
        P.dma("sp", self.X[0:NS, NT, :], self.din["xs"][:, :], slot=("x", NT), writes=[("X", NT)])
        for t in range(NT):
            P.dma("sp", self.X[:, t, :], self.din["xp"][t * 128:(t + 1) * 128, :], slot=("x", t), writes=[("X", t)])
        self.plan_weights()
        only = self.only
        self.norm_setup()
        nn = lambda gi, le=False: (lambda t, part="ab": self.norm_tile(gi, t, part, lnexp=le))
        allt = list(range(NT)) + [NT]
        if "ret" in only:
            self.P.region = 2
            self.retention(nn(1))
            self.P.region = 0
        else:
            self.free("rcos", "rsin")
            for t in allt:
                self.norm_tile(1, t)
        if "dumpx" in self.dbg:
            for t in range(NT):
                P.dma("sp", self.dout["yp"][t * 128:(t + 1) * 128, :], self.X[:, t, :], slot="yo", reads=[("X", t)], new_gen=False)
        if "swa" in only:
            self.swa_consts()
        if "ffn0" in only:
            self.ffn(0, last=False, next_norm=nn(2))
        else:
            for t in allt:
                self.norm_tile(2, t)
        if "swa" in only:
            self.P.region = 1
            self.swa(nn(3, True))
            self.P.region = 0
        else:
            for t in allt:
                self.norm_tile(3, t)
        if "ffn1" in only:
            self.ffn(1, last=True)
        self.resolve_evictions()
        P.sched_regions = SCHED_REGIONS
        P.build()
        P.close()
        return self.nc


_CACHE = {}


def _get_program(stage=99):
    if stage not in _CACHE:
        consts = build_consts()
        m = Mega(stage=stage)
        nc = m.build(consts)
        _CACHE[stage] = (nc, consts)
    return _CACHE[stage]


def kernel(x_prompt, x_sample, state_ret, cache_swa_k, cache_swa_v, norm_mix, norm_ffn,
           w_ret_in, w_ret_out, w_swa_in, w_swa_out, swa_q_norm, swa_k_norm, swa_sinks,
           w_ffn_in, w_ffn_out, _stage=99):
    nc, consts = _get_program(_stage)
    f = lambda a: np.ascontiguousarray(np.asarray(a, dtype=np.float32))
    shared = {
        "norm_mix": f(norm_mix), "norm_ffn": f(norm_ffn),
        "w_ret_in": f(w_ret_in[0]), "w_ret_out": f(w_ret_out[0]), "w_swa_in": f(w_swa_in[0]), "w_swa_out": f(w_swa_out[0]),
        "swa_q_norm": f(swa_q_norm), "swa_k_norm": f(swa_k_norm), "swa_sinks": f(swa_sinks),
        "w_ffn_in0": f(w_ffn_in[0]), "w_ffn_in1": f(w_ffn_in[1]), "w_ffn_out0": f(w_ffn_out[0]), "w_ffn_out1": f(w_ffn_out[1]),
    }
    shared.update(consts)
    in_maps = []
    for c in range(8):
        m = dict(shared)
        m["xp"] = f(x_prompt[c])
        m["xs"] = f(x_sample[NB * c:NB * (c + 1)]).reshape(NS, D)
        m["state"] = f(state_ret[0, NB * c:NB * (c + 1)])
        m["ck"] = f(cache_swa_k[0, NB * c:NB * (c + 1)]).reshape(NB, 128, 256)
        m["cv"] = f(cache_swa_v[0, NB * c:NB * (c + 1)]).reshape(NB, 128, 256)
        in_maps.append(m)
    res = run_bass_kernel_spmd(nc, in_maps, core_ids=list(range(8)))
    rs = res.results
    y_p = np.stack([r["yp"] for r in rs], 0).astype(np.float32)
    y_s = np.concatenate([r["ys"].reshape(NB, TS, D) for r in rs], 0).astype(np.float32)
    st_p = np.stack([r["sp_out"] for r in rs], 0)[None].astype(np.float32)
    st_s = np.concatenate([r["ss_out"] for r in rs], 0)[None].astype(np.float32)
    k_p = np.stack([r["kp"].reshape(128, KVH, HD) for r in rs], 0)[None].astype(np.float32)
    v_p = np.stack([r["vp"].reshape(128, KVH, HD) for r in rs], 0)[None].astype(np.float32)
    k_s = np.concatenate([r["ks"].reshape(NB, 128, KVH, HD) for r in rs], 0)[None].astype(np.float32)
    v_s = np.concatenate([r["vs"].reshape(NB, 128, KVH, HD) for r in rs], 0)[None].astype(np.float32)
    return (y_p, y_s, st_p, st_s, k_p, v_p, k_s, v_s)
```

```python
from contextlib import ExitStack
import numpy as np
import concourse.bass as bass
import concourse.mybir as mybir
from concourse.bass_utils import run_bass_kernel_spmd

F32 = mybir.dt.float32
BF16 = mybir.dt.bfloat16
AF = mybir.ActivationFunctionType
ALU = mybir.AluOpType
AX = mybir.AxisListType

ENGS = ("pe", "act", "dve", "pool", "sp")

D = 1024
SEQ = 2048
NT = 16
NS = 64
NB = 16
TS = 4
PAST = 16384
RH, RDK, RDV = 4, 256, 512
QH, KVH, HD = 16, 4, 64
DFF = 2816
NU = DFF // 128
EPS = 1e-6
TOKS = SEQ + NS


class Op:
    __slots__ = ("eng", "emit", "deps", "idx", "sig", "sigcnt", "slot", "cnt", "is_dma", "waits", "pre_slot_wait", "seq", "dur", "tbl", "reg")

    def __init__(self, eng, emit):
        self.eng = eng
        self.emit = emit
        self.deps = []
        self.idx = -1
        self.sig = False
        self.sigcnt = 0
        self.slot = None
        self.cnt = 0
        self.is_dma = False
        self.waits = []
        self.pre_slot_wait = 0
        self.seq = 0
        self.dur = 0.3
        self.tbl = None
        self.reg = 0


class Prog:
    def __init__(self, nc):
        self.nc = nc
        self.ops = {e: [] for e in ENGS}
        self.last_w = {}
        self.readers = {}
        self.slot_cnt = {}
        self.stack = ExitStack()
        self.sems = {}
        self.slot_sems = {}
        self.inherit = {}
        self.keys_of = {}

    def sbuf(self, name, shape, dtype):
        return self.stack.enter_context(self.nc.sbuf_tensor(name, list(shape), dtype))

    def psum(self, name, shape, dtype):
        return self.stack.enter_context(self.nc.psum_tensor(name, list(shape), dtype))

    @staticmethod
    def _name(k):
        return k[0] if isinstance(k, tuple) else k

    def _first_touch(self, k, deps):
        n = self._name(k)
        ks = self.keys_of.setdefault(n, set())
        if k not in ks:
            ks.add(k)
            inh = self.inherit.get(n)
            if inh:
                deps.extend(inh)

    def all_ops_of(self, name):
        out = []
        for k in self.keys_of.get(name, ()):
            w = self.last_w.get(k)
            if w is not None:
                out.append(w)
            out.extend(self.readers.get(k, ()))
        out.extend(self.inherit.get(name, ()))
        return out

    def _track(self, op, reads, writes):
        deps = op.deps
        for k in reads:
            self._first_touch(k, deps)
            w = self.last_w.get(k)
            if w is not None:
                deps.append(w)
            self.readers.setdefault(k, []).append(op)
        for k in writes:
            self._first_touch(k, deps)
            w = self.last_w.get(k)
            if w is not None:
                deps.append(w)
            rs = self.readers.get(k)
            if rs:
                deps.extend(rs)
            self.last_w[k] = op
            self.readers[k] = []

    def op(self, eng, emit, reads=(), writes=(), after=(), dur=0.3, tbl=None):
        o = Op(eng, emit)
        o.dur = dur
        o.tbl = tbl
        self.nseq = getattr(self, "nseq", 0) + 1
        o.seq = self.nseq
        o.reg = getattr(self, "region", 0)
        o.idx = len(self.ops[eng])
        self.ops[eng].append(o)
        o.deps.extend(after)
        self._track(o, reads, writes)
        return o

    def dma(self, eng, out, in_, slot, reads=(), writes=(), new_gen=True, after=(), **kw):
        def emit(e, out=out, in_=in_, kw=kw):
            return e.dma_start(out=out, in_=in_, **kw)
        o = Op(eng, emit)
        o.is_dma = True
        self.nseq = getattr(self, "nseq", 0) + 1
        o.seq = self.nseq
        try:
            nb = 1
            for d_ in out.shape:
                nb *= d_
            o.dur = 2.0 + nb * 4 / 150e3
        except Exception:
            o.dur = 3.0
        o.reg = getattr(self, "region", 0)
        o.idx = len(self.ops[eng])
        self.ops[eng].append(o)
        o.deps.extend(after)
        c = self.slot_cnt.get(slot, 0)
        if new_gen:
            o.pre_slot_wait = c
        o.slot = slot
        o.cnt = c + 1
        self.slot_cnt[slot] = c + 1
        self._track(o, reads, writes)
        return o

    def check_deadlock(self):
        ptr = {e: 0 for e in ENGS}
        progress = True
        while progress:
            progress = False
            for e in ENGS:
                ops = self.ops[e]
                while ptr[e] < len(ops):
                    o = ops[ptr[e]]
                    ok = True
                    for d in o.deps:
                        if d is o:
                            continue
                        if d.eng == e:
                            if d.idx >= o.idx:
                                raise RuntimeError("forward same-engine dep")
                            continue
                        if d.idx >= ptr[d.eng]:
                            ok = False
                            break
                    if not ok:
                        break
                    ptr[e] += 1
                    progress = True
        stuck = {e: (ptr[e], len(self.ops[e])) for e in ENGS if ptr[e] < len(self.ops[e])}
        if stuck:
            raise RuntimeError(f"static deadlock: {stuck}")

    def schedule(self, region=1, window=600, fixed_extra=()):
        import heapq
        full = {e: list(self.ops[e]) for e in ENGS}
        sel = {e: [o for o in full[e] if o.reg == region] for e in ENGS}
        if not any(sel.values()):
            return
        keep = set(id(o) for e in ENGS for o in sel[e])
        allops = [o for e in ENGS for o in sel[e]]
        saved_ops = self.ops
        self.ops = sel
        succ = {id(o): [] for o in allops}
        ndep = {}
        for o in allops:
            ds = {id(d): d for d in o.deps if d is not o and id(d) in keep}
            ndep[id(o)] = len(ds)
            for d in ds.values():
                succ[id(d)].append(o)
        ready_t = {id(o): 0.0 for o in allops}
        fixed = {"sp", "pool"} | set(fixed_extra)
        pending = {e: list(self.ops[e]) for e in ENGS}
        ptr = {e: 0 for e in ENGS}
        avail = {e: [] for e in ENGS}
        opmap = {id(o): o for o in allops}
        for o in allops:
            if ndep[id(o)] == 0:
                heapq.heappush(avail[o.eng], (o.seq, id(o)))
        t_eng = {e: 0.0 for e in ENGS}
        last_tbl = [None]
        order = {e: [] for e in ENGS}
        done = 0
        total = len(allops)
        minseq = {e: 0 for e in ENGS}
        events = []
        finished = set()
        while done < total:
            progressed = False
            for e in ENGS:
                if not avail[e]:
                    continue
                if e in fixed:
                    nxt = pending[e][ptr[e]] if ptr[e] < len(pending[e]) else None
                    if nxt is None or ndep[id(nxt)] != 0:
                        continue
                    cand = nxt
                    avail[e] = [x for x in avail[e] if x[1] != id(cand)]
                    heapq.heapify(avail[e])
                else:
                    now = t_eng[e]
                    lo = avail[e][0][0]
                    best = None
                    for (sq_, i_) in sorted(avail[e])[:24]:
                        o_ = opmap[i_]
                        if sq_ > lo + window:
                            break
                        r_ = max(ready_t[i_], now)
                        pen = 0.0
                        if e == "act" and o_.tbl is not None and last_tbl[0] is not None and o_.tbl != last_tbl[0]:
                            pen = 1.3
                        key = (r_ + pen, sq_)
                        if best is None or key < best[0]:
                            best = (key, o_)
                    cand = best[1]
                    avail[e].remove((cand.seq, id(cand)))
                    heapq.heapify(avail[e])
                st = max(t_eng[e], ready_t[id(cand)])
                if e == "act" and cand.tbl is not None:
                    if last_tbl[0] is not None and cand.tbl != last_tbl[0]:
                        st += 1.3
                    last_tbl[0] = cand.tbl
                if cand.is_dma:
                    t_eng[e] = st + 0.1
                    fin = st + cand.dur
                else:
                    t_eng[e] = st + cand.dur
                    fin = t_eng[e]
                order[e].append(cand)
                if e in fixed:
                    ptr[e] += 1
                done += 1
                progressed = True
                for s_ in succ[id(cand)]:
                    lat = 0.05 if s_.eng == cand.eng else 0.25
                    ready_t[id(s_)] = max(ready_t[id(s_)], fin + lat)
                    ndep[id(s_)] -= 1
                    if ndep[id(s_)] == 0:
                        heapq.heappush(avail[s_.eng], (s_.seq, id(s_)))
            if not progressed:
                raise RuntimeError("scheduler stuck")
        self.ops = saved_ops
        for e in ENGS:
            assert len(order[e]) == len(sel[e])
            if not sel[e]:
                continue
            it = iter(order[e])
            self.ops[e] = [next(it) if id(o) in keep else o for o in full[e]]
            for i, o in enumerate(self.ops[e]):
                o.idx = i
        self.est_time = max(t_eng.values())

    def build(self, resched=True):
        nc = self.nc
        if resched:
            for args_ in getattr(self, "sched_regions", ((1, 600),)):
                self.schedule(*args_)
        self.check_deadlock()
        for e in ENGS:
            known = {}
            for o in self.ops[e]:
                need = {}
                if o.is_dma and o.pre_slot_wait > 0:
                    need[("slot", o.slot)] = o.pre_slot_wait
                for d in o.deps:
                    if d is o:
                        continue
                    if d.is_dma:
                        k = ("slot", d.slot)
                        v = d.cnt
                    else:
                        if d.eng == "pe" and e == "pe" and not o.is_dma:
                            continue
                        k = ("eng", d.eng)
                        v = d.idx
                        if d.eng == e and d.idx >= o.idx:
                            raise RuntimeError("forward same-engine dep")
                    if need.get(k, -1) < v:
                        need[k] = v
                o.waits = []
                for k, v in need.items():
                    if known.get(k, -1) >= v:
                        continue
                    known[k] = v
                    o.waits.append((k, v))
                    if k[0] == "eng":
                        self.ops[k[1]][v].sig = True
                o.deps = None
        for e in ENGS:
            c = 0
            for o in self.ops[e]:
                if o.sig:
                    c += 1
                o.sigcnt = c
            assert c < 60000, (e, c)
        for e in ENGS:
            self.sems[e] = self.stack.enter_context(nc.semaphore("s_" + e))
        for s in self.slot_cnt:
            self.slot_sems[s] = self.stack.enter_context(nc.semaphore("d_" + str(s)))
        block = self.stack.enter_context(nc.Block())
        final_slots = dict(self.slot_cnt)
        last_sig = {e: (self.ops[e][-1].sigcnt if self.ops[e] else 0) for e in ENGS}

        def run(e, eng):
            for o in self.ops[e]:
                for (k, v) in o.waits:
                    if k[0] == "eng":
                        eng.wait_ge(self.sems[k[1]], self.ops[k[1]][v].sigcnt)
                    else:
                        eng.wait_ge(self.slot_sems[k[1]], 16 * v)
                ins = o.emit(eng)
                if o.is_dma:
                    ins.then_inc(self.slot_sems[o.slot], 16)
                elif o.sig:
                    ins.then_inc(self.sems[e], 1)
            if e == "sp":
                for s, c in final_slots.items():
                    eng.wait_ge(self.slot_sems[s], 16 * c)

        @block.tensor
        def _(eng):
            run("pe", eng)

        @block.scalar
        def _(eng):
            run("act", eng)

        @block.vector
        def _(eng):
            run("dve", eng)

        @block.gpsimd
        def _(eng):
            run("pool", eng)

        @block.sync
        def _(eng):
            run("sp", eng)

    def close(self):
        self.stack.close()


def _ret_lg():
    return np.log(np.float32(1.0) - np.float32(2.0) ** (np.float32(-5.0) - np.arange(RH, dtype=np.float32))).astype(np.float32)


def build_consts():
    c = {}
    lg = _ret_lg()
    inv = (np.float32(1.0) / np.power(np.float32(10000.0), np.linspace(0.0, 1.0, RDK // 2, dtype=np.float32))).astype(np.float32)
    pos = np.concatenate([np.arange(SEQ, dtype=np.float32),
                          np.tile(np.float32(PAST) + np.arange(TS, dtype=np.float32), NB)]).astype(np.float32)
    ang = (inv[:, None] * pos[None, :]).astype(np.float32)
    c["c_rcos"] = np.cos(ang).astype(np.float32)
    c["c_rsin"] = np.sin(ang).astype(np.float32)
    i = np.arange(128, dtype=np.float32)
    diff = i[None, :] - i[:, None]
    decT = np.zeros((128, RH, 128), np.float32)
    for h in range(RH):
        decT[:, h, :] = np.where(diff >= 0, np.exp(np.maximum(diff, 0.0) * lg[h]), 0.0)
    c["c_decT"] = decT.astype(np.float32)
    qrow = np.zeros((RH, 128 + NS), np.float32)
    tt = np.tile(np.arange(TS, dtype=np.float32), NB)
    for h in range(RH):
        qrow[h, :128] = np.exp((i + 1.0) * lg[h])
        qrow[h, 128:] = np.exp((tt + 1.0) * lg[h])
    c["c_qrow"] = np.broadcast_to(qrow[None], (128, RH, 128 + NS)).astype(np.float32).copy()
    kcol = np.zeros((128, 2 * RH), np.float32)
    for h in range(RH):
        kcol[:, h] = np.exp((127.0 - i) * lg[h])
        kcol[:NS, RH + h] = np.exp((TS - 1.0 - tt) * lg[h])
    c["c_kcol"] = kcol
    bj = np.arange(NS) // TS
    tj = (np.arange(NS) % TS).astype(np.float32)
    same = (bj[:, None] == bj[None, :])
    dd = tj[None, :] - tj[:, None]
    sdec = np.zeros((NS, RH, NS), np.float32)
    for h in range(RH):
        sdec[:, h, :] = np.where(same & (dd >= 0), np.exp(np.maximum(dd, 0.0) * lg[h]), 0.0)
    c["c_sdec"] = sdec
    cm = (np.arange(NB)[:, None] == bj[None, :]).astype(np.float32)
    c["c_colmask"] = np.broadcast_to(cm[None], (128, NB, NS)).astype(np.float32).copy()
    c["c_rowmask"] = (bj[:, None] == np.arange(NB)[None, :]).astype(np.float32)
    c["c_ident"] = np.eye(128, dtype=np.float32)
    sinv = (np.float32(1.0) / np.power(np.float32(10000.0), np.arange(0, HD, 2, dtype=np.float32) / np.float32(HD))).astype(np.float32)
    spos = np.zeros((128, NT + 1), np.float32)
    for b in range(NT):
        spos[:, b] = b * 128 + np.arange(128)
    spos[:NS, NT] = np.float32(PAST) + tt
    sang = (spos[:, :, None] * sinv[None, None, :]).astype(np.float32)
    c["c_scos"] = np.cos(sang).astype(np.float32)
    c["c_ssin"] = np.sin(sang).astype(np.float32)
    kk = np.arange(128)
    bm = np.zeros((128, 2, 128), np.float32)
    bm[:, 0, :] = (kk[:, None] >= kk[None, :])
    bm[:, 1, :] = (kk[:, None] <= kk[None, :])
    c["c_band"] = bm
    c["c_cmask"] = (kk[:, None] >= np.arange(TS)[None, :]).astype(np.float32)
    c["c_nmask"] = (same & (dd >= 0)).astype(np.float32)
    return c


CONST_SHAPES = None


class Builder:
    swa_blocks = NT
    dbg = frozenset()

    def __init__(self, stage=99, debug=False):
        self.stage = stage
        self.debug = debug
        nc = bass.Bass("TRN2", target_bir_lowering=False)
        self.nc = nc
        self.P = Prog(nc)
        self.din = {}
        self.dout = {}
        self.uid = 0

    def inp(self, name, shape):
        self.din[name] = self.nc.dram_tensor(name, list(shape), F32, kind="ExternalInput").ap()
        return self.din[name]

    def outp(self, name, shape):
        self.dout[name] = self.nc.dram_tensor(name, list(shape), F32, kind="ExternalOutput").ap()
        return self.dout[name]

    def init_mem(self, total_f32):
        self.R = self.P.sbuf("R", [128, total_f32], F32)
        self.R_total = total_f32 * 4
        self.live = {}
        self.freed = []
        self.top = 0

    def alloc(self, name, free_shape, dtype, at=None):
        esz = 2 if dtype == BF16 else 4
        n = int(np.prod(free_shape))
        size = (n * esz + 63) // 64 * 64
        if at is None:
            off = self._find(size)
        else:
            off = at
        assert off + size <= self.R_total, ("SBUF overflow", name, off, size, self.R_total)
        for (o2, s2) in self.live.values():
            assert off + size <= o2 or o2 + s2 <= off, ("overlap live", name)
        self.live[name] = (off, size)
        self.peak = max(getattr(self, "peak", 0), off + size)
        inh = []
        for (o2, s2, n2) in self.freed:
            if not (off + size <= o2 or o2 + s2 <= off):
                inh.extend(self.P.all_ops_of(n2))
        if inh:
            self.P.inherit[name] = inh
        assert name not in self.P.keys_of, ("buffer name reused", name)
        v = self.R[:, off // 4:(off + size) // 4]
        if dtype == BF16:
            v = v.bitcast(BF16)
        v = v[:, 0:n]
        if len(free_shape) == 2:
            v = v.rearrange("p (a b) -> p a b", a=free_shape[0])
        elif len(free_shape) == 3:
            v = v.rearrange("p (a b c) -> p a b c", a=free_shape[0], b=free_shape[1])
        elif len(free_shape) == 4:
            v = v.rearrange("p (a b c d) -> p a b c d", a=free_shape[0], b=free_shape[1], c=free_shape[2])
        return v

    def _find(self, size):
        iv = sorted(self.live.values())
        pos = 0
        for (o, s) in iv:
            if o - pos >= size:
                return pos
            pos = max(pos, o + s)
        return pos

    def free(self, *names):
        for name in names:
            off, size = self.live.pop(name)
            self.freed.append((off, size, name))

    @staticmethod
    def _fsz(ap):
        n = 1
        for d_ in ap.shape[1:]:
            n *= d_
        return n

    def mm(self, out, lhsT, rhs, start, stop, reads, writes):
        return self.P.op("pe", lambda e: e.matmul(out, lhsT=lhsT, rhs=rhs, start=start, stop=stop), reads, writes,
                         dur=0.06 + self._fsz(out) / 1700.0)

    def tr(self, out, in_, ident, reads, writes):
        return self.P.op("pe", lambda e: e.transpose(out=out, in_=in_, identity=ident), reads, writes, dur=0.11)

    _TBL = {"Silu": "silu", "Sqrt": "sqrt", "Exp": "exp", "Ln": "exp"}

    def act(self, out, in_, func, reads, writes, **kw):
        tbl = self._TBL.get(getattr(func, "name", str(func)).split(".")[-1])
        return self.P.op("act", lambda e: e.activation(out=out, in_=in_, func=func, **kw), reads, writes,
                         dur=0.22 + self._fsz(out) / 1000.0, tbl=tbl)

    def _vd(self, out, eng):
        return (0.12 + self._fsz(out) / 900.0) * (2.0 if eng == "pool" else 1.0)

    def tt(self, out, in0, in1, op, reads, writes, eng="dve"):
        return self.P.op(eng, lambda e: e.tensor_tensor(out=out, in0=in0, in1=in1, op=op), reads, writes, dur=self._vd(out, eng))

    def ts(self, out, in0, s1, s2, op0, op1, reads, writes, eng="dve"):
        if op1 is None:
            return self.P.op(eng, lambda e: e.tensor_scalar(out=out, in0=in0, scalar1=s1, scalar2=None, op0=op0), reads, writes,
                             dur=self._vd(out, eng))
        return self.P.op(eng, lambda e: e.tensor_scalar(out=out, in0=in0, scalar1=s1, scalar2=s2, op0=op0, op1=op1), reads, writes,
                         dur=self._vd(out, eng))

    def stt(self, out, in0, scalar, in1, op0, op1, reads, writes, eng="dve"):
        return self.P.op(eng, lambda e: e.scalar_tensor_tensor(out=out, in0=in0, scalar=scalar, in1=in1, op0=op0, op1=op1), reads, writes,
                         dur=self._vd(out, eng))

    def cp(self, out, in_, reads, writes, eng="dve"):
        return self.P.op(eng, lambda e: e.tensor_copy(out=out, in_=in_), reads, writes, dur=self._vd(out, eng))

    def recip(self, out, in_, reads, writes):
        return self.P.op("dve", lambda e: e.reciprocal(out=out, in_=in_), reads, writes, dur=self._vd(out, "dve"))

    def memset(self, ap, val, writes, eng="dve"):
        return self.P.op(eng, lambda e: e.memset(ap, val), (), writes, dur=0.1 + self._fsz(ap) / 2000.0)

    def reduce(self, out, in_, op, reads, writes):
        return self.P.op("dve", lambda e: e.tensor_reduce(out=out, in_=in_, axis=AX.X, op=op), reads, writes,
                         dur=0.12 + self._fsz(in_) / 900.0)

    def init_psum(self):
        self.PS = self.P.psum("PS", [128, 8, 512], F32)
        self.ps_next = 0
        self.ps_pinned = set()

    def bank(self, n=1):
        def consumed(b):
            k = ("ps", b)
            return self.P.last_w.get(k) is None or len(self.P.readers.get(k, ())) > 0
        for _ in range(16):
            i = self.ps_next
            if i + n <= 8 and all(((i + j) not in self.ps_pinned) and consumed(i + j) for j in range(n)):
                self.ps_next = (i + n) % 8
                return i, [("ps", i + j) for j in range(n)]
            self.ps_next = (i + 1) % 8
        raise RuntimeError("no free PSUM bank (all pending consumption)")

    def psf(self, i, n=1):
        return self.PS[:, i:i + n, :].rearrange("p a b -> p (a b)")

    def psb(self, i):
        return self.PS[:, i, :].bitcast(BF16)


W_ELEMS = 24576
FFN_GROUPS = [4, 4, 4, 4, 3, 3]
R_F32 = 53150
SCHED_REGIONS = ((1, 600), (2, 300, ("pe",)))


class Mega(Builder):
    def plan_weights(self):
        d = self.din
        pieces = []

        def colpiece(name, w, c0, n):
            pieces.append((name, [8, n], w[:, c0:c0 + n].rearrange("(c p) n -> p c n", p=128)))

        def rowpiece(name, w, r0, nchunk):
            pieces.append((name, [nchunk, D], w[r0:r0 + nchunk * 128, :].rearrange("(c p) n -> p c n", p=128)))

        wri, wro = d["w_ret_in"], d["w_ret_out"]
        only = self.only
        for h in range(RH if "ret" in only else 0):
            colpiece(f"rq{h}", wri, h * RDK, RDK)
            colpiece(f"rk{h}", wri, RH * RDK + h * RDK, RDK)
            colpiece(f"rv{h}", wri, 2 * RH * RDK + h * RDV, RDV)
            colpiece(f"rg{h}", wri, 2 * RH * RDK + RH * RDV + h * RDV, RDV)
            rowpiece(f"ro{h}", wro, h * RDV, RDV // 128)

        def ffn(l):
            wi, wo = d[f"w_ffn_in{l}"], d[f"w_ffn_out{l}"]
            u0 = 0
            for gi, G in enumerate(FFN_GROUPS):
                colpiece(f"fg{l}_{gi}", wi, u0 * 128, G * 128)
                colpiece(f"fu{l}_{gi}", wi, DFF + u0 * 128, G * 128)
                rowpiece(f"fo{l}_{gi}", wo, u0 * 128, G)
                u0 += G
        if "ffn0" in only:
            ffn(0)
        if "swa" in only:
            for j in range(3):
                colpiece(f"si{j}", d["w_swa_in"], j * 512, 512)
            rowpiece("so", d["w_swa_out"], 0, 8)
        if "ffn1" in only:
            ffn(1)
        self.W = {}
        self.wkey = {}
        live = []
        ptr = 0
        self.evictions = []
        prev_w = None
        for i, (name, shp, src) in enumerate(pieces):
            n = int(np.prod(shp))
            if ptr + n > W_ELEMS:
                ptr = 0
            ev = [x for x in live if not (ptr + n <= x[0] or x[0] + x[1] <= ptr)]
            live = [x for x in live if x not in ev]
            live.append((ptr, n, name))
            v = self.WA[:, ptr:ptr + n].rearrange("p (a b) -> p a b", a=shp[0])
            self.W[name] = v
            key = ("W", name)
            self.wkey[name] = key
            op = self.P.dma("pool", v, src, slot=("w", i % 8), writes=[key], after=([prev_w] if prev_w is not None else []))
            prev_w = op
            for x in ev:
                self.evictions.append((op, x[2]))
            ptr += n

    def resolve_evictions(self):
        for op, name in self.evictions:
            key = ("W", name)
            w = self.P.last_w.get(key)
            rs = self.P.readers.get(key, [])
            assert op.deps is not None
            op.deps.extend(rs)
            if w is not None:
                op.deps.append(w)

    def norm_setup(self):
        P = self.P
        self.gainT = self.alloc("gainT", [4, 8], F32)
        srcs = [("norm_mix", 0), ("norm_ffn", 0), ("norm_mix", 1), ("norm_ffn", 1)]
        for i, (nm, l) in enumerate(srcs):
            P.dma("sp", self.gainT[:, i, :], self.din[nm][l:l + 1, :].rearrange("o (c p) -> p (o c)", p=128), slot="gT",
                  writes=["gainT"], new_gen=False, allow_slow_non_contiguous=True)
        self.hbs = [self.alloc(f"hb{i}", [D], BF16) for i in range(2)]
        self.nst = self.alloc("nst", [2, 4], F32)
        self.norm_cnt = 0
        self.norm_ctx = {}

    def norm_tile(self, gi, t, part="ab", lnexp=False):
        n = 128 if t < NT else NS
        if "a" in part:
            i = self.norm_cnt % 2
            self.norm_cnt += 1
            self.norm_ctx[t] = i
            hb, hk = self.hbs[i], f"hb{i}"
            st = self.nst[:, i, :]
            sk = ("nst", i)
            xk = ("X", t)
            x = self.X[0:n, t, :]
            self.act(hb[0:n, :], x, AF.Square, [xk], [hk, sk], accum_out=st[0:n, 0:1])
            if lnexp:
                self.act(st[0:n, 1:2], st[0:n, 0:1], AF.Ln, [sk, "epsb"], [sk], scale=1.0 / D, bias=self.epsb[0:n, 0:1])
                self.act(st[0:n, 2:3], st[0:n, 1:2], AF.Exp, [sk], [sk], scale=-0.5)
            else:
                self.act(st[0:n, 1:2], st[0:n, 0:1], AF.Sqrt, [sk, "epsb"], [sk], scale=1.0 / D, bias=self.epsb[0:n, 0:1])
                self.recip(st[0:n, 2:3], st[0:n, 1:2], [sk], [sk])
            self.ts(hb[0:n, :], x, st[0:n, 2:3], None, ALU.mult, None, [xk, sk], [hk])
        if "b" in part:
            i = self.norm_ctx.pop(t)
            hb, hk = self.hbs[i], f"hb{i}"
            bi, bk = self.bank()
            pb = self.psb(bi).rearrange("p (a b) -> p a b", a=8)
            for c in range(8):
                self.tr(pb[:, c, 0:n], hb[0:n, c * 128:(c + 1) * 128], self.ident[0:n, 0:n], [hk, "ident"], bk)
            self.tt(self.HT[:, :, t * 128:t * 128 + n], pb[:, :, 0:n], self.gainT[:, gi, :].unsqueeze(2).broadcast_to([128, 8, n]), ALU.mult,
                    bk + ["gainT"], [("HT", t)])

    def retention(self, next_norm):
        P = self.P
        lg = _ret_lg()
        g128 = [float(np.exp(np.float32(128.0) * lg[h])) for h in range(RH)]
        g4 = [float(np.exp(np.float32(4.0) * lg[h])) for h in range(RH)]
        A = self.alloc
        rcos, rsin = self.rcos, self.rsin
        decT = A("decT", [128], F32); qrow = A("qrow", [128 + NS], F32); kcol = A("kcol", [2 * RH], F32)
        sdec = A("sdec", [NS], F32); rowmask = A("rowmask", [NB], F32)
        c = self.din
        P.dma("sp", kcol, c["c_kcol"][:, :], slot="c4", writes=["kcol"])
        P.dma("sp", rowmask[0:NS], c["c_rowmask"][:, :], slot="c6", writes=["rowmask"])
        qr = [A(f"qr{i}", [2, 512], BF16) for i in range(2)]
        kr = [A(f"kr{i}", [2, 512], BF16) for i in range(2)]
        qd = [A(f"qd{i}", [2, 128], BF16) for i in range(2)]
        ta = A("ta", [512], F32); tb = A("tb", [512], F32)
        vc = [A(f"vc{i}", [512], BF16) for i in range(2)]; sg4 = A("sg4", [4, 512], BF16)
        sTm = [A(f"sTm{i}", [128], BF16) for i in range(2)]
        ktok = [A(f"ktok{i}", [256], BF16) for i in range(2)]
        gated = A("gated", [512], BF16); oTg = A("oTg", [4, 128], BF16)
        Sf = A("Sf", [2, 512], F32); Sb = A("Sb", [2, 512], BF16)
        rst = A("rst", [2, 4], F32)
        qr_s = A("qr_s", [2, NS], BF16); kr_s = A("kr_s", [2, NS], BF16); qd_s = A("qd_s", [2, NS], BF16)
        v_s = A("v_s", [512], BF16); sg_s = A("sg_s", [512], BF16); ktok_s = A("ktok_s", [256], BF16)
        sTm_s = A("sTm_s", [NS], BF16)
        qpad = [A(f"qpad{i}", [2, NS], BF16) for i in range(2)]
        kpad = [A(f"kpad{i}", [256], BF16) for i in range(2)]
        S0 = [A(f"S0_{i}", [2, 512], F32) for i in range(2)]
        S0b = A("S0b", [2, 512], BF16)
        for i in range(2):
            self.memset(qpad[i], 0.0, [f"qpad{i}"])
        HT, X, ident = self.HT, self.X, self.ident
        state, ss_out, sp_out = self.din["state"], self.dout["ss_out"], self.dout["sp_out"]
        gcnt = [0]
        pend_nb = []

        def rope(ps_i, keys, n, tok0, out, outkey, scale):
            x1 = self.psf(ps_i)[:, 0:n]; x2 = self.psf(ps_i + 1)[:, 0:n]
            cs = rcos[:, tok0:tok0 + n]; sn = rsin[:, tok0:tok0 + n]
            a = ta[:, 0:n]; b_ = tb[:, 0:n]
            self.stt(a, x1, scale, cs, ALU.mult, ALU.mult, [keys[0], "rcos"], ["ta"])
            self.stt(b_, x2, scale, sn, ALU.mult, ALU.mult, [keys[1], "rsin"], ["tb"])
            self.tt(out[:, 0, 0:n], a, b_, ALU.subtract, ["ta", "tb"], [outkey])
            self.stt(a, x2, scale, cs, ALU.mult, ALU.mult, [keys[1], "rcos"], ["ta"])
            self.stt(b_, x1, scale, sn, ALU.mult, ALU.mult, [keys[0], "rsin"], ["tb"])
            self.tt(out[:, 1, 0:n], a, b_, ALU.add, ["ta", "tb"], [outkey])

        def proj_fm(wname, n, tok0, htkeys):
            bi, bk = self.bank(2)
            w = self.W[wname]
            for oc in range(2):
                for dc in range(8):
                    self.mm(self.psf(bi + oc)[:, 0:n], w[:, dc, oc * 128:(oc + 1) * 128], HT[:, dc, tok0:tok0 + n],
                            dc == 0, dc == 7, [self.wkey[wname]] + htkeys, [bk[oc]])
            return bi, bk

        def proj_tm(wname, n, tok0, htkeys):
            bi, bk = self.bank(1)
            w = self.W[wname]
            for dc in range(8):
                self.mm(self.psf(bi)[0:n, :], HT[:, dc, tok0:tok0 + n], w[:, dc, :], dc == 0, dc == 7,
                        [self.wkey[wname]] + htkeys, bk)
            return bi, bk

        def gn_elem(n, o_bank, o_keys, sgbuf, sgkey):
            i = gcnt[0] % 2
            gcnt[0] += 1
            r_ = rst[:, i, :]; rk = ("rst", i)
            self.act(gated[0:n, :], self.psf(o_bank)[0:n, :], AF.Square, o_keys, ["gated", rk], accum_out=r_[0:n, 0:1])
            self.act(r_[0:n, 1:2], r_[0:n, 0:1], AF.Sqrt, [rk, "epsb"], [rk], scale=1.0 / RDV, bias=self.epsb[0:n, 0:1])
            self.recip(r_[0:n, 2:3], r_[0:n, 1:2], [rk], [rk])
            self.stt(gated[0:n, :], self.psf(o_bank)[0:n, :], r_[0:n, 2:3], sgbuf[0:n, :], ALU.mult, ALU.mult,
                     o_keys + [rk, sgkey], ["gated"])

        def gn_tr(n):
            ti, tk = self.bank()
            pb = self.psb(ti).rearrange("p (a b) -> p a b", a=8)
            for ec in range(4):
                self.tr(pb[:, ec, 0:n], gated[0:n, ec * 128:(ec + 1) * 128], ident[0:n, 0:n], ["gated", "ident"], tk)
            self.act(oTg[:, :, 0:n], pb[:, 0:4, 0:n], AF.Copy, tk, ["oTg"])

        def gn_out(h, n, xt):
            yi, yk = self.bank(2)
            wo = self.W[f"ro{h}"]
            for hf in range(2):
                for ec in range(4):
                    self.mm(self.psf(yi + hf)[0:n, :], oTg[:, ec, 0:n], wo[:, ec, hf * 512:(hf + 1) * 512], ec == 0, ec == 3,
                            ["oTg", self.wkey[f"ro{h}"]], [yk[hf]])
            xv = X[0:n, xt, :]
            self.tt(xv, xv, self.psf(yi, 2)[0:n, :], ALU.add, [("X", xt)] + yk, [("X", xt)])
            if h == RH - 1 and "nonext" not in self.dbg:
                if n == 128:
                    next_norm(xt, "a")
                    pend_nb.append(xt)
                else:
                    next_norm(xt)

        for h in range(RH):
            P.dma("sp", decT, c["c_decT"][:, h, :], slot="c2", writes=["decT"])
            P.dma("sp", qrow, c["c_qrow"][:, h, :], slot="c3", writes=["qrow"])
            P.dma("sp", sdec[0:NS], c["c_sdec"][:, h, :], slot="c5", writes=["sdec"])
            if h == 0:
                self.norm_tile(0, NT)
                for t in range(NT if "normup" in self.dbg else 4):
                    self.norm_tile(0, t)
            st_keys = [("HT", NT)]
            bi, bk = proj_fm(f"rq{h}", NS, SEQ, st_keys)
            rope(bi, bk, NS, SEQ, qr_s, "qr_s", 1.0)
            bi, bk = proj_fm(f"rk{h}", NS, SEQ, st_keys)
            rope(bi, bk, NS, SEQ, kr_s, "kr_s", 1.0 / 16.0)
            self.tt(qd_s, qr_s, qrow[:, 128:128 + NS].unsqueeze(1).broadcast_to([128, 2, NS]), ALU.mult, ["qr_s", "qrow"], ["qd_s"])
            bi, bk = proj_tm(f"rv{h}", NS, SEQ, st_keys)
            self.act(v_s[0:NS, :], self.psf(bi)[0:NS, :], AF.Copy, bk, ["v_s"])
            bi, bk = proj_tm(f"rg{h}", NS, SEQ, st_keys)
            self.act(sg_s[0:NS, :], self.psf(bi)[0:NS, :], AF.Silu, bk, ["sg_s"])
            ti, tk = self.bank()
            pb = self.psb(ti).rearrange("p (a b) -> p a b", a=8)
            for hf in range(2):
                self.tr(pb[0:NS, hf, :], kr_s[:, hf, :], ident, ["kr_s", "ident"], tk)
            self.act(ktok_s[0:NS, :], pb[0:NS, 0:2, :].rearrange("p a b -> p (a b)"), AF.Copy, tk + ["kcol"], ["ktok_s"],
                     scale=kcol[0:NS, RH + h:RH + h + 1])
            si, sk_ = self.bank()
            for hf in range(2):
                self.mm(self.psf(si)[0:NS, 0:NS], kr_s[:, hf, :], qr_s[:, hf, :], hf == 0, hf == 1, ["kr_s", "qr_s"], sk_)
            self.tt(sTm_s[0:NS, :], self.psf(si)[0:NS, 0:NS], sdec[0:NS, :], ALU.mult, sk_ + ["sdec"], ["sTm_s"])
            osi, osk = self.bank()
            self.ps_pinned.add(osi)
            self.mm(self.psf(osi)[0:NS, :], sTm_s[0:NS, :], v_s[0:NS, :], True, False, ["sTm_s", "v_s"], osk)
            for b in range(2):
                P.dma("sp", S0[b % 2], state[b, h].rearrange("(c p) e -> p c e", p=128), slot=("s0", b % 2),
                      writes=[f"S0_{b % 2}"])
            self.memset(Sf, 0.0, ["Sf"])
            self.memset(Sb, 0.0, ["Sb"])
            prev = None
            for s in range(4):
                tok0 = s * 512
                htk = [("HT", 4 * s + i) for i in range(4)]
                qb = qr[s % 2]; kb = kr[s % 2]; qbk = f"qr{s % 2}"; kbk = f"kr{s % 2}"
                bi, bk = proj_fm(f"rq{h}", 512, tok0, htk)
                rope(bi, bk, 512, tok0, qb, qbk, 1.0)
                bi, bk = proj_fm(f"rk{h}", 512, tok0, htk)
                rope(bi, bk, 512, tok0, kb, kbk, 1.0 / 16.0)
                for cc in range(4):
                    bi, bk = proj_tm(f"rg{h}", 128, tok0 + cc * 128, [("HT", 4 * s + cc)])
                    self.act(sg4[:, cc, :], self.psf(bi), AF.Silu, bk, [("sg4", cc)])
                for cc in range(4):
                    cidx = s * 4 + cc
                    c0 = cc * 128
                    t0 = cidx * 128
                    hk1 = [("HT", cidx)]
                    i2 = cidx % 2
                    qdb = qd[i2]; qdk = f"qd{i2}"
                    stb = sTm[i2]; stk = f"sTm{i2}"
                    ktb = ktok[i2]; ktk = f"ktok{i2}"
                    vcb = vc[i2]; vck = f"vc{i2}"
                    sgb = sg4[:, cc, :]; sgk = ("sg4", cc)
                    b = cidx
                    r = b % 2
                    s0k = f"S0_{r}"; qpk = f"qpad{r}"; kpk = f"kpad{r}"
                    ti, tk = self.bank()
                    pb = self.psb(ti).rearrange("p (a b) -> p a b", a=8)
                    for hf in range(2):
                        self.tr(pb[:, hf, :], kb[:, hf, c0:c0 + 128], ident, [kbk, "ident"], tk)
                    self.act(ktb, pb[:, 0:2, :].rearrange("p a b -> p (a b)"), AF.Copy, tk + ["kcol"], [ktk], scale=kcol[:, h:h + 1])
                    self.act(S0b.rearrange("p a b -> p (a b)"), S0[r].rearrange("p a b -> p (a b)"), AF.Copy, [s0k], ["S0b"])
                    self.cp(qpad[r][:, :, 4 * b:4 * b + 4], qd_s[:, :, 4 * b:4 * b + 4], ["qd_s"], [qpk])
                    self.ts(kpad[r][0:NS, :], ktok_s[0:NS, :], rowmask[0:NS, b:b + 1], None, ALU.mult, None, ["ktok_s", "rowmask"], [kpk])
                    bi, bk = proj_tm(f"rv{h}", 128, t0, hk1)
                    self.act(vcb, self.psf(bi), AF.Copy, bk, [vck])
                    ui, uk = self.bank(2)
                    for hf in range(2):
                        self.mm(self.psf(ui + hf), ktb[:, hf * 128:(hf + 1) * 128], vcb, True, True, [ktk, vck], [uk[hf]])
                    Sfv = Sf.rearrange("p a b -> p (a b)")
                    self.stt(Sfv, Sfv, g128[h], self.psf(ui, 2), ALU.mult, ALU.add, ["Sf"] + uk, ["Sf"])
                    for hf in range(2):
                        self.mm(self.psf(osi)[0:NS, :], qpad[r][:, hf, :], S0b[:, hf, :], False, (b == NB - 1 and hf == 1),
                                [qpk, "S0b"], osk)
                    self.memset(qpad[r][:, :, 4 * b:4 * b + 4], 0.0, [qpk])
                    ui, uk = self.bank(2)
                    for hf in range(2):
                        self.mm(self.psf(ui + hf), kpad[r][0:NS, hf * 128:(hf + 1) * 128], v_s[0:NS, :], True, True, [kpk, "v_s"], [uk[hf]])
                    S0v = S0[r].rearrange("p a b -> p (a b)")
                    self.stt(S0v, S0v, g4[h], self.psf(ui, 2), ALU.mult, ALU.add, [s0k] + uk, [s0k])
                    P.dma("sp", ss_out[b, h].rearrange("(c p) e -> p c e", p=128), S0[r], slot=("sso", r), reads=[s0k])
                    if b + 2 < NB:
                        P.dma("sp", S0[r], state[b + 2, h].rearrange("(c p) e -> p c e", p=128), slot=("s0", r), writes=[s0k])
                    self.tt(qdb, qb[:, :, c0:c0 + 128], qrow[:, 0:128].unsqueeze(1).broadcast_to([128, 2, 128]), ALU.mult,
                            [qbk, "qrow"], [qdk])
                    si, sk_ = self.bank()
                    for hf in range(2):
                        self.mm(self.psf(si)[:, 0:128], kb[:, hf, c0:c0 + 128], qb[:, hf, c0:c0 + 128], hf == 0, hf == 1, [kbk, qbk], sk_)
                    self.tt(stb, self.psf(si)[:, 0:128], decT, ALU.mult, sk_ + ["decT"], [stk])
                    while len(pend_nb) > 1 or (pend_nb and pend_nb[0] < cidx - 1):
                        next_norm(pend_nb.pop(0), "b")
                    if prev is not None:
                        gn_tr(128)
                    oi, ok_ = self.bank()
                    self.mm(self.psf(oi), stb, vcb, True, False, [stk, vck], ok_)
                    for hf in range(2):
                        self.mm(self.psf(oi), qdb[:, hf, :], Sb[:, hf, :], False, hf == 1, [qdk, "Sb"], ok_)
                    if cidx < NT - 1:
                        self.act(Sb.rearrange("p a b -> p (a b)"), Sfv, AF.Copy, ["Sf"], ["Sb"])
                    gn_elem(128, oi, ok_, sgb, sgk)
                    if prev is not None:
                        gn_out(h, 128, prev)
                    prev = cidx
                    if "nopipe" in self.dbg:
                        gn_tr(128)
                        gn_out(h, 128, prev)
                        prev = None
                    if h == 0 and s < 3 and "normup" not in self.dbg:
                        self.norm_tile(0, 4 * (s + 1) + cc)
            if prev is not None:
                gn_tr(128)
                gn_out(h, 128, prev)
            while pend_nb:
                next_norm(pend_nb.pop(0), "b")
            P.dma("sp", sp_out[h].rearrange("(c p) e -> p c e", p=128), Sf, slot="spo", reads=["Sf"])
            self.ps_pinned.discard(osi)
            gn_elem(NS, osi, osk, sg_s, "sg_s")
            gn_tr(NS)
            gn_out(h, NS, NT)
        self.free("rcos", "rsin", "decT", "qrow", "kcol", "sdec", "rowmask", "qr0", "qr1", "kr0", "kr1", "qd0", "qd1", "ta", "tb",
                  "vc0", "vc1", "sg4", "sTm0", "sTm1", "ktok0", "ktok1", "gated", "oTg", "Sf", "Sb", "rst", "qr_s", "kr_s", "qd_s",
                  "v_s", "sg_s", "ktok_s", "sTm_s", "qpad0", "qpad1", "kpad0", "kpad1", "S0_0", "S0_1", "S0b")

    def ffn(self, l, last, next_norm=None):
        P = self.P
        A = self.alloc
        t = f"f{l}"
        sgl = [A(f"sgl{t}{i}", [512], BF16) for i in range(2)]
        aT = [A(f"aT{t}{i}", [4, 512], BF16) for i in range(2)]
        HT, X = self.HT, self.X
        tiles = [(s * 512, 512, [("HT", 4 * s + i) for i in range(4)]) for s in range(4)] + [(SEQ, NS, [("HT", NT)])]
        ng = len(FFN_GROUPS)
        uidx = 0
        for gi, G in enumerate(FFN_GROUPS):
            wg, wu, wo = self.W[f"fg{l}_{gi}"], self.W[f"fu{l}_{gi}"], self.W[f"fo{l}_{gi}"]
            kg, ku, ko = self.wkey[f"fg{l}_{gi}"], self.wkey[f"fu{l}_{gi}"], self.wkey[f"fo{l}_{gi}"]
            for si, (tok0, n, htk) in enumerate(tiles):
                ab = aT[si % 2]; abk = f"aT{t}{si % 2}"
                for u in range(G):
                    sb = sgl[uidx % 2]; sbk = f"sgl{t}{uidx % 2}"
                    uidx += 1
                    gi_, gk = self.bank()
                    for dc in range(8):
                        self.mm(self.psf(gi_)[:, 0:n], wg[:, dc, u * 128:(u + 1) * 128], HT[:, dc, tok0:tok0 + n], dc == 0, dc == 7,
                                [kg] + htk, gk)
                    ui_, uk = self.bank()
                    for dc in range(8):
                        self.mm(self.psf(ui_)[:, 0:n], wu[:, dc, u * 128:(u + 1) * 128], HT[:, dc, tok0:tok0 + n], dc == 0, dc == 7,
                                [ku] + htk, uk)
                    self.act(sb[:, 0:n], self.psf(gi_)[:, 0:n], AF.Silu, gk, [sbk])
                    self.tt(ab[:, u, 0:n], sb[:, 0:n], self.psf(ui_)[:, 0:n], ALU.mult, [sbk] + uk, [(abk, u)])
                pend_norm = []
                for tt_ in range((n + 127) // 128):
                    m = min(128, n - tt_ * 128)
                    xt = (tok0 // 128 + tt_) if tok0 < SEQ else NT
                    yi, yk = self.bank(2)
                    for hf in range(2):
                        for u in range(G):
                            self.mm(self.psf(yi + hf)[0:m, :], ab[:, u, tt_ * 128:tt_ * 128 + m], wo[:, u, hf * 512:(hf + 1) * 512],
                                    u == 0, u == G - 1, [(abk, u), ko], [yk[hf]])
                    xv = X[0:m, xt, :]
                    self.tt(xv, xv, self.psf(yi, 2)[0:m, :], ALU.add, [("X", xt)] + yk, [("X", xt)])
                    if next_norm is not None and gi == ng - 1:
                        pend_norm.append(xt)
                    if last and gi == ng - 1:
                        if xt < NT:
                            P.dma("sp", self.dout["yp"][xt * 128:(xt + 1) * 128, :], xv, slot="yo", reads=[("X", xt)], new_gen=False)
                        else:
                            P.dma("sp", self.dout["ys"][:, :], xv, slot="yo", reads=[("X", xt)], new_gen=False)
                for xt in pend_norm:
                    next_norm(xt)
        self.free(*[f"sgl{t}{i}" for i in range(2)], *[f"aT{t}{i}" for i in range(2)])

    def swa_consts(self):
        P = self.P
        A = self.alloc
        c = self.din
        scos = A("scos", [NT + 1, 32], F32); ssin = A("ssin", [NT + 1, 32], F32)
        gqk = A("gqk", [20, 64], F32)
        esink = A("esink", [16], F32); esinkp = A("esinkp", [16], F32)
        ones = A("ones", [128], BF16)
        P.dma("sp", scos, c["c_scos"][:, :, :], slot="c0", writes=["scos"])
        P.dma("sp", ssin, c["c_ssin"][:, :, :], slot="c1", writes=["ssin"])
        for hh in range(16):
            P.dma("sp", gqk[:, hh, :], c["swa_q_norm"][0:1, :].partition_broadcast(128), slot="c5", writes=[("gqk", hh)], new_gen=False)
        for hh in range(4):
            P.dma("sp", gqk[:, 16 + hh, :], c["swa_k_norm"][0:1, :].partition_broadcast(128), slot="c5", writes=[("gqk", 16 + hh)], new_gen=False)
        P.dma("sp", esink, c["swa_sinks"][0:1, :].partition_broadcast(128), slot="c6", writes=["esink"])
        for hk_ in range(4):
            for hf in range(2):
                for g_ in range(2):
                    hd = 4 * hk_ + 2 * g_ + hf
                    col = hk_ * 4 + hf * 2 + g_
                    P.dma("sp", esinkp[:, col:col + 1], c["swa_sinks"][0:1, hd:hd + 1].partition_broadcast(128), slot="c7",
                          writes=[("esinkp", col)], new_gen=False)
        self.memset(ones, 1.0, ["ones"])
        self.swa_c = (scos, ssin, gqk, esink, esinkp, ones)

    def swa(self, next_norm):
        P = self.P
        A = self.alloc
        c = self.din
        HT, X, ident = self.HT, self.X, self.ident
        scos, ssin, gqk, esink, esinkp, ones = self.swa_c
        band, cmask, nmask = self.band, self.cmask, self.nmask
        gqk_keys = [("gqk", i) for i in range(20)]
        esp_keys = [("esinkp", i) for i in range(16)]
        self.act(esink, esink, AF.Exp, ["esink"], ["esink"])
        self.act(esinkp, esinkp, AF.Exp, esp_keys, ["esinkp"])
        qk = A("qk", [20, 64], F32); sq = A("sq", [20, 64], BF16); qkr = A("qkr", [20, 64], F32)
        ta = A("sta", [20, 32], F32); tb = A("stb", [20, 32], F32)
        sst = A("sst", [3, 20], F32)
        q_r = A("q_r", [16, 64], BF16); kdup = A("kdup", [4, 2, 64], BF16)
        vf = A("vf", [256], F32)
        vaug = [A(f"vaug{i}", [4, 65], BF16) for i in range(3)]
        qTs = [A(f"qT{i}", [8, 128], BF16) for i in range(2)]
        kTd = [A(f"kTd{i}", [4, 128], BF16) for i in range(3)]
        pT = [A(f"pT{i}", [2, 512], BF16) for i in range(2)]
        o_n = A("o_n", [16, 64], BF16); oT = A("oT", [8, 128], BF16)
        dn = A("dn", [2, 4], F32)
        for i in range(3):
            self.memset(vaug[i][:, :, 64:65], 1.0, [f"vaug{i}"])
        wsi = [self.W[f"si{j}"] for j in range(3)]
        ksi = [self.wkey[f"si{j}"] for j in range(3)]
        wso, kso = self.W["so"], self.wkey["so"]

        def front_proj(n, tok0, htk):
            bi, bk = self.bank(3)
            for j in range(3):
                for dc in range(8):
                    self.mm(self.psf(bi + j)[0:n, :], HT[:, dc, tok0:tok0 + n], wsi[j][:, dc, :], dc == 0, dc == 7, [ksi[j]] + htk, [bk[j]])
            return bi, bk

        def front_parts(n, tbl, bi, bk, vdst_ops):
            ps = self.psf(bi, 3)
            qkf = qk.rearrange("p a b -> p (a b)")
            sqf = sq.rearrange("p a b -> p (a b)")
            x1 = qk[0:n, :, 0:32]; x2 = qk[0:n, :, 32:64]
            cs = scos[0:n, tbl, :].unsqueeze(1).broadcast_to([n, 20, 32]); sn = ssin[0:n, tbl, :].unsqueeze(1).broadcast_to([n, 20, 32])

            def p0():
                self.act(qkf[0:n, :], ps[0:n, 0:1280], AF.Copy, bk, ["qk"])
                vdst_ops(ps, bk)
                self.act(sqf[0:n, :], qkf[0:n, :], AF.Square, ["qk"], ["sq"])

            def p1():
                self.reduce(sst[0:n, 0, :], sq[0:n], ALU.add, ["sq"], [("sst", 0)])
                self.act(sst[0:n, 1, :], sst[0:n, 0, :], AF.Ln, [("sst", 0), "epsb"], [("sst", 1)], scale=1.0 / HD, bias=self.epsb[0:n, 0:1])
                self.act(sst[0:n, 2, :], sst[0:n, 1, :], AF.Exp, [("sst", 1)], [("sst", 2)], scale=-0.5)
                self.tt(qk[0:n], qk[0:n], sst[0:n, 2, :].unsqueeze(2).broadcast_to([n, 20, 64]), ALU.mult, ["qk", ("sst", 2)], ["qk"])
                self.tt(qk[0:n], qk[0:n], gqk[0:n], ALU.mult, ["qk"] + gqk_keys, ["qk"])

            def p2():
                self.tt(ta[0:n], x1, cs, ALU.mult, ["qk", "scos"], ["sta"])
                self.tt(tb[0:n], x2, sn, ALU.mult, ["qk", "ssin"], ["stb"])
                self.tt(qkr[0:n, :, 0:32], ta[0:n], tb[0:n], ALU.subtract, ["sta", "stb"], ["qkr"])

            def p3():
                self.tt(ta[0:n], x2, cs, ALU.mult, ["qk", "scos"], ["sta"])
                self.tt(tb[0:n], x1, sn, ALU.mult, ["qk", "ssin"], ["stb"])
                self.tt(qkr[0:n, :, 32:64], ta[0:n], tb[0:n], ALU.add, ["sta", "stb"], ["qkr"])
                self.act(q_r[0:n], qkr[0:n, 0:16, :], AF.Copy, ["qkr"], ["q_r"])
                self.act(kdup[0:n], qkr[0:n, 16:20, :].unsqueeze(2).broadcast_to([n, 4, 2, 64]), AF.Copy, ["qkr"], ["kdup"])
            return [p0, p1, p2, p3]

        def qkv_front(n, tok0, htk, tbl, vdst_ops):
            bi, bk = front_proj(n, tok0, htk)
            for p in front_parts(n, tbl, bi, bk, vdst_ops):
                p()

        nblk = self.swa_blocks

        def mk_vdst(n_):
            vb = vaug[n_ % 3]; vbk = f"vaug{n_ % 3}"

            def vdst(ps, bk):
                self.act(vb[:, :, 0:64], ps[:, 1280:1536].rearrange("p (a b) -> p a b", a=4), AF.Copy, bk, [vbk])
                if n_ == NT - 1:
                    self.act(vf, ps[:, 1280:1536], AF.Copy, bk, ["vf"])
                    P.dma("sp", self.dout["vp"][:, :], vf, slot="vpo", reads=["vf"])
            return vdst

        def front_tr(n_):
            qT = qTs[n_ % 2]; qTk = f"qT{n_ % 2}"
            ktb = kTd[n_ % 3]; ktk = f"kTd{n_ % 3}"
            if n_ == NT - 1:
                P.dma("sp", self.dout["kp"][:, :], qkr[:, 16:20, :].rearrange("p a b -> p (a b)"), slot="kpo", reads=["qkr"])
            ti, tk = self.bank()
            pb = self.psb(ti).rearrange("p (a b) -> p a b", a=8)
            for cc in range(8):
                self.tr(pb[:, cc, :], q_r[:, 2 * cc:2 * cc + 2, :].rearrange("p a b -> p (a b)"), ident, ["q_r", "ident"], tk)
            self.act(qT, pb, AF.Copy, tk, [qTk])
            ti, tk = self.bank()
            pb = self.psb(ti).rearrange("p (a b) -> p a b", a=8)
            for hk_ in range(4):
                self.tr(pb[:, hk_, :], kdup[:, hk_, :, :].rearrange("p a b -> p (a b)"), ident, ["kdup", "ident"], tk)
            self.act(ktb, pb[:, 0:4, :], AF.Copy, tk, [ktk])

        def scores(n_, hk_):
            qT = qTs[n_ % 2]; qTk = f"qT{n_ % 2}"
            kbs = ([n_ - 1] if n_ > 0 else []) + [n_]
            nkb = len(kbs)
            pt = pT[hk_ % 2]; ptk = f"pT{hk_ % 2}"
            si, sk_ = self.bank(2)
            w_ = nkb * 256
            for kbi, kb_ in enumerate(kbs):
                for hf in range(2):
                    self.mm(self.psf(si + hf)[:, kbi * 256:(kbi + 1) * 256],
                            kTd[kb_ % 3][hf * 64:(hf + 1) * 64, hk_, :], qT[hf * 64:(hf + 1) * 64, 2 * hk_:2 * hk_ + 2, :],
                            True, True, [f"kTd{kb_ % 3}", qTk], [sk_[hf]])
            m0 = 0 if n_ > 0 else 1
            for hf in range(2):
                self.act(pt[:, hf, 0:w_], self.psf(si + hf)[:, 0:w_], AF.Exp, [sk_[hf]], [ptk], scale=HD ** -0.5)
                pv4 = pt[:, hf, 0:w_].rearrange("p (k g q) -> p k g q", g=2, q=128)
                self.tt(pv4, pv4, band[:, m0:m0 + nkb, :].unsqueeze(2).broadcast_to([128, nkb, 2, 128]), ALU.mult, [ptk, "band"], [ptk])

        def pvnorm(n_, hk_):
            kbs = ([n_ - 1] if n_ > 0 else []) + [n_]
            nkb = len(kbs)
            pt = pT[hk_ % 2]; ptk = f"pT{hk_ % 2}"
            pvi, pvk = self.bank()
            pv = self.psf(pvi)[:, 0:260].rearrange("p (a b) -> p a b", a=4)
            for j in range(4):
                hf, g_ = j % 2, j // 2
                for kbi, kb_ in enumerate(kbs):
                    self.mm(pv[:, j, :], pt[:, hf, kbi * 256 + g_ * 128:kbi * 256 + g_ * 128 + 128], vaug[kb_ % 3][:, hk_, :],
                            kbi == 0, kbi == nkb - 1, [ptk, f"vaug{kb_ % 3}"], pvk)
            self.tt(dn[:, 0, :], pv[:, :, 64], esink[:, 4 * hk_:4 * hk_ + 4], ALU.add, pvk + ["esink"], [("dn", 0)])
            self.recip(dn[:, 1, :], dn[:, 0, :], [("dn", 0)], [("dn", 1)])
            self.tt(o_n[:, 4 * hk_:4 * hk_ + 4, :], pv[:, :, 0:64], dn[:, 1, :].unsqueeze(2).broadcast_to([128, 4, 64]), ALU.mult,
                    pvk + [("dn", 1)], ["o_n"])

        def o_tr(n_):
            ti, tk = self.bank()
            pb = self.psb(ti).rearrange("p (a b) -> p a b", a=8)
            for cc in range(8):
                self.tr(pb[:, cc, :], o_n[:, 2 * cc:2 * cc + 2, :].rearrange("p a b -> p (a b)"), ident, ["o_n", "ident"], tk)
            self.act(oT, pb, AF.Copy, tk, ["oT"])

        def o_mm(n_):
            yi, yk = self.bank(2)
            for hf in range(2):
                for cc in range(8):
                    self.mm(self.psf(yi + hf), oT[:, cc, :], wso[:, cc, hf * 512:(hf + 1) * 512], cc == 0, cc == 7, ["oT", kso], [yk[hf]])
            xv = X[:, n_, :]
            self.tt(xv, xv, self.psf(yi, 2), ALU.add, [("X", n_)] + yk, [("X", n_)])
            next_norm(n_, "a")

        if nblk > 0:
            bi, bk = front_proj(128, 0, [("HT", 0)])
            for p in front_parts(128, 0, bi, bk, mk_vdst(0)):
                p()
            front_tr(0)
        for n_ in range(nblk + 1):
            cur = n_ < nblk
            nxt = n_ + 1 < nblk
            prv = n_ - 1 if n_ > 0 else None
            parts = None
            if cur:
                scores(n_, 0)
            if prv is not None:
                o_tr(prv)
            for hk_ in range(4):
                if cur:
                    if hk_ + 1 < 4:
                        scores(n_, hk_ + 1)
                    pvnorm(n_, hk_)
                if hk_ == 0 and nxt:
                    bi, bk = front_proj(128, (n_ + 1) * 128, [("HT", n_ + 1)])
                    parts = front_parts(128, n_ + 1, bi, bk, mk_vdst(n_ + 1))
                    parts[0]()
                if hk_ == 1 and prv is not None:
                    o_mm(prv)
                if hk_ == 1 and nxt:
                    parts[1]()
                if hk_ == 2 and nxt:
                    parts[2]()
                if hk_ == 2 and prv is not None:
                    next_norm(prv, "b")
                if hk_ == 3 and nxt:
                    parts[3]()
            if nxt:
                front_tr(n_ + 1)

        if "nosample" in self.dbg:
            self.free("scos", "ssin", "gqk", "esink", "esinkp", "ones", "qk", "sq", "qkr", "sta", "stb", "sst", "q_r", "kdup",
                      "vf", "vaug0", "vaug1", "vaug2", "qT0", "qT1", "kTd0", "kTd1", "kTd2", "pT0", "pT1", "o_n", "oT", "dn")
            return
        ck, cv = c["ck"], c["cv"]
        ks_out, vs_out = self.dout["ks"], self.dout["vs"]
        P.dma("sp", ks_out[:, 0:124, :], ck[:, 4:128, :], slot="kso")
        P.dma("sp", vs_out[:, 0:124, :], cv[:, 4:128, :], slot="vso")
        self.free("vf", "vaug0", "vaug1", "vaug2", "qT0", "qT1", "kTd0", "kTd1", "kTd2", "pT0", "pT1", "o_n", "oT", "dn")
        vfs = A("vfs", [256], F32); vdn = A("vdn", [4, 2, 64], BF16)
        qT_s = A("qT_s", [8, NS], BF16); kTd_s = A("kTd_s", [4, NS], BF16)
        pn = A("pn", [16, NS], BF16)
        kc = [A(f"kc{i}", [256], F32) for i in range(2)]; vcc = [A(f"vcc{i}", [256], F32) for i in range(2)]
        kcd = A("kcd", [4, 2, 64], BF16); vcd = [A(f"vcd{i}", [4, 2, 64], BF16) for i in range(2)]
        kTc = A("kTc", [4, 128], BF16); pc = [A(f"pc{i}", [16, TS], BF16) for i in range(2)]
        dsb = A("dsb", [16, NS], F32); o_sn = A("o_sn", [16, NS], BF16)

        def vdst_s(ps, bk):
            self.act(vfs[0:NS, :], ps[0:NS, 1280:1536], AF.Copy, bk, ["vfs"])
            self.act(vdn[0:NS], ps[0:NS, 1280:1536].rearrange("p (a b) -> p a b", a=4).unsqueeze(2).broadcast_to([NS, 4, 2, 64]), AF.Copy,
                     bk, ["vdn"])
        qkv_front(NS, SEQ, [("HT", NT)], NT, vdst_s)
        for b in range(NB):
            P.dma("sp", ks_out[b, 124:128, :], qkr[4 * b:4 * b + 4, 16:20, :].rearrange("p a b -> p (a b)"), slot="kso2", reads=["qkr"],
                  new_gen=False)
            P.dma("sp", vs_out[b, 124:128, :], vfs[4 * b:4 * b + 4, :], slot="vso2", reads=["vfs"], new_gen=False)
        ti, tk = self.bank()
        pb = self.psb(ti).rearrange("p (a b) -> p a b", a=8)
        for cc in range(8):
            self.tr(pb[:, cc, 0:NS], q_r[0:NS, 2 * cc:2 * cc + 2, :].rearrange("p a b -> p (a b)"), ident[0:NS, 0:NS], ["q_r", "ident"], tk)
        self.act(qT_s, pb[:, :, 0:NS], AF.Copy, tk, ["qT_s"])
        ti, tk = self.bank()
        pb = self.psb(ti).rearrange("p (a b) -> p a b", a=8)
        for hk_ in range(4):
            self.tr(pb[:, hk_, 0:NS], kdup[0:NS, hk_, :, :].rearrange("p a b -> p (a b)"), ident[0:NS, 0:NS], ["kdup", "ident"], tk)
        self.act(kTd_s, pb[:, 0:4, 0:NS], AF.Copy, tk, ["kTd_s"])
        si, sk_ = self.bank(2)
        for hf in range(2):
            for hk_ in range(4):
                self.mm(self.psf(si + hf)[0:NS, hk_ * 128:(hk_ + 1) * 128], kTd_s[hf * 64:(hf + 1) * 64, hk_, :],
                        qT_s[hf * 64:(hf + 1) * 64, 2 * hk_:2 * hk_ + 2, :], True, True, ["kTd_s", "qT_s"], [sk_[hf]])
        for hk_ in range(4):
            pv_ = pn[0:NS, 4 * hk_:4 * hk_ + 4, :].rearrange("p a (b x t) -> p (a b) x t", x=4, t=TS)
            for hf in range(2):
                self.act(pv_[:, :, 2 * hf:2 * hf + 2, :],
                         self.psf(si + hf)[0:NS, hk_ * 128:(hk_ + 1) * 128].rearrange("p (g b t) -> p b g t", g=2, t=TS), AF.Exp,
                         [sk_[hf]], ["pn"], scale=HD ** -0.5)
            self.tt(pv_, pv_, nmask[0:NS, :].rearrange("p (b t) -> p b t", t=TS).unsqueeze(2).broadcast_to([NS, NB, 4, TS]), ALU.mult,
                    ["pn", "nmask"], ["pn"])
        oi, ok_ = self.bank(2)
        di, dk = self.bank(2)
        for j in range(2):
            self.ps_pinned.add(oi + j); self.ps_pinned.add(di + j)
        oacc = self.psf(oi, 2)
        dacc = self.psf(di, 2)
        pnf = pn.rearrange("p a b -> p (a b)")
        for hk_ in range(4):
            self.mm(oacc[:, hk_ * 256:(hk_ + 1) * 256], vdn[0:NS, hk_, :, :].rearrange("p a b -> p (a b)"), pnf[0:NS, hk_ * 256:(hk_ + 1) * 256],
                    hk_ % 2 == 0, False, ["vdn", "pn"], [ok_[hk_ // 2]])
        for j in range(2):
            self.mm(dacc[:, j * 512:(j + 1) * 512], ones[0:NS, :], pnf[0:NS, j * 512:(j + 1) * 512], True, False, ["ones", "pn"], [dk[j]])
        for b in range(2):
            P.dma("sp", kc[b % 2], ck[b, :, :], slot=("kc", b % 2), writes=[f"kc{b % 2}"])
            P.dma("sp", vcc[b % 2], cv[b, :, :], slot=("vc", b % 2), writes=[f"vcc{b % 2}"])
        for b in range(NB):
            r = b % 2
            lastb = (b == NB - 1)
            self.act(kcd, kc[r].rearrange("p (a b) -> p a b", a=4).unsqueeze(2).broadcast_to([128, 4, 2, 64]), AF.Copy, [f"kc{r}"], ["kcd"])
            self.act(vcd[r], vcc[r].rearrange("p (a b) -> p a b", a=4).unsqueeze(2).broadcast_to([128, 4, 2, 64]), AF.Copy, [f"vcc{r}"],
                     [f"vcd{r}"])
            if b + 2 < NB:
                P.dma("sp", kc[r], ck[b + 2, :, :], slot=("kc", r), writes=[f"kc{r}"])
                P.dma("sp", vcc[r], cv[b + 2, :, :], slot=("vc", r), writes=[f"vcc{r}"])
            ti, tk = self.bank()
            pb = self.psb(ti).rearrange("p (a b) -> p a b", a=8)
            for hk_ in range(4):
                self.tr(pb[:, hk_, :], kcd[:, hk_, :, :].rearrange("p a b -> p (a b)"), ident, ["kcd", "ident"], tk)
            self.act(kTc, pb[:, 0:4, :], AF.Copy, tk, ["kTc"])
            si, sk_ = self.bank(2)
            for hf in range(2):
                for hk_ in range(4):
                    self.mm(self.psf(si + hf)[:, hk_ * 8:(hk_ + 1) * 8].rearrange("p (a b) -> p a b", a=2), kTc[hf * 64:(hf + 1) * 64, hk_, :],
                            qT_s[hf * 64:(hf + 1) * 64, 2 * hk_:2 * hk_ + 2, 4 * b:4 * b + 4], True, True, ["kTc", "qT_s"], [sk_[hf]])
            pcb = pc[r]; pck = f"pc{r}"
            pc4 = pcb.rearrange("p (k h g) t -> p k h (g t)", k=4, h=2)
            for hf in range(2):
                self.act(pc4[:, :, hf, :], self.psf(si + hf)[:, 0:32].rearrange("p (k x) -> p k x", k=4), AF.Exp, [sk_[hf]], [pck],
                         scale=HD ** -0.5)
            self.tt(pcb, pcb, cmask[:, :].unsqueeze(1).broadcast_to([128, 16, TS]), ALU.mult, [pck, "cmask"], [pck])
            pcf = pcb.rearrange("p a b -> p (a b)")
            for hk_ in range(4):
                c0 = hk_ * 256 + b * 16
                self.mm(oacc[:, c0:c0 + 16], vcd[r][:, hk_, :, :].rearrange("p a b -> p (a b)"), pcf[:, hk_ * 16:(hk_ + 1) * 16],
                        False, lastb and hk_ % 2 == 1, [f"vcd{r}", pck], [ok_[hk_ // 2]])
                self.mm(dacc[:, c0:c0 + 16], ones, pcf[:, hk_ * 16:(hk_ + 1) * 16], False, lastb and hk_ % 2 == 1, ["ones", pck], [dk[hk_ // 2]])
        for j in range(2):
            self.ps_pinned.discard(oi + j); self.ps_pinned.discard(di + j)
        dsf = dsb.rearrange("p a b -> p (a b)")
        for hk_ in range(4):
            sl = slice(hk_ * 256, (hk_ + 1) * 256)
            self.tt(dsf[:, sl].rearrange("p (b x t) -> p b x t", x=4, t=TS), dacc[:, sl].rearrange("p (b x t) -> p b x t", x=4, t=TS),
                    esinkp[:, 4 * hk_:4 * hk_ + 4].unsqueeze(1).unsqueeze(3).broadcast_to([128, NB, 4, TS]), ALU.add,
                    [dk[hk_ // 2], "esinkp"], ["dsb"])
        self.recip(dsb, dsb, ["dsb"], ["dsb"])
        for hk_ in range(4):
            sl = slice(hk_ * 256, (hk_ + 1) * 256)
            self.tt(o_sn[:, 4 * hk_:4 * hk_ + 4, :].rearrange("p x (b t) -> p b x t", t=TS),
                    oacc[:, sl].rearrange("p (b x t) -> p b x t", x=4, t=TS), dsf[:, sl].rearrange("p (b x t) -> p b x t", x=4, t=TS),
                    ALU.mult, [ok_[hk_ // 2], "dsb"], ["o_sn"])
        xv = X[0:NS, NT, :]
        for hf2 in range(2):
            yi, yk = self.bank(2)
            for hf in range(2):
                idx = 0
                for hk_ in range(4):
                    for g_ in range(2):
                        cch = 2 * hk_ + g_
                        lh = o_sn[hf * 64:(hf + 1) * 64, hk_ * 4 + hf * 2 + g_, :]
                        self.mm(self.psf(yi + hf)[0:NS, :], lh, wso[hf * 64:(hf + 1) * 64, cch, hf2 * 512:(hf2 + 1) * 512],
                                idx == 0, idx == 7, ["o_sn", kso], [yk[hf]])
                        idx += 1
            xh = xv[:, hf2 * 512:(hf2 + 1) * 512]
            for hf in range(2):
                self.tt(xh, xh, self.psf(yi + hf)[0:NS, :], ALU.add, [("X", NT), yk[hf]], [("X", NT)])
        next_norm(NT)
        self.free("scos", "ssin", "gqk", "esink", "esinkp", "ones", "qk", "sq", "qkr", "sta", "stb", "sst",
                  "q_r", "kdup",
                  "vfs", "vdn", "qT_s", "kTd_s", "pn", "kc0", "kc1", "vcc0", "vcc1", "kcd", "vcd0", "vcd1", "kTc", "pc0", "pc1",
                  "dsb", "o_sn")

    def build(self, consts):
        P = self.P
        inp, outp = self.inp, self.outp
        inp("xp", [SEQ, D]); inp("xs", [NS, D]); inp("state", [NB, RH, RDK, RDV]); inp("ck", [NB, 128, 256]); inp("cv", [NB, 128, 256])
        inp("norm_mix", [2, D]); inp("norm_ffn", [2, D])
        inp("w_ret_in", [D, 6144]); inp("w_ret_out", [2048, D]); inp("w_swa_in", [D, 1536]); inp("w_swa_out", [D, D])
        inp("swa_q_norm", [1, 64]); inp("swa_k_norm", [1, 64]); inp("swa_sinks", [1, 16])
        inp("w_ffn_in0", [D, 2 * DFF]); inp("w_ffn_in1", [D, 2 * DFF]); inp("w_ffn_out0", [DFF, D]); inp("w_ffn_out1", [DFF, D])
        for k, v in consts.items():
            inp(k, v.shape)
        outp("yp", [SEQ, D]); outp("ys", [NS, D]); outp("sp_out", [RH, RDK, RDV]); outp("ss_out", [NB, RH, RDK, RDV])
        outp("kp", [128, 256]); outp("vp", [128, 256]); outp("ks", [NB, 128, 256]); outp("vs", [NB, 128, 256])
        st = self.stage
        if isinstance(st, (set, frozenset, tuple, list)):
            self.only = set(st)
        else:
            self.only = set(["ret", "ffn0", "swa", "ffn1"][:min(int(st), 4)])
        self.init_mem(R_F32)
        self.init_psum()
        self.X = self.alloc("Xbuf", [NT + 1, D], F32)
        self.HT = self.alloc("HTbuf", [8, TOKS], BF16)
        self.WA = self.alloc("WAbuf", [W_ELEMS], BF16)
        self.ident = self.alloc("ident", [128], BF16)
        self.epsb = self.alloc("epsb", [1], F32)
        self.memset(self.epsb, EPS, ["epsb"])
        P.dma("pool", self.ident, self.din["c_ident"][:, :], slot="cid", writes=["ident"])
        self.band = self.alloc("band", [2, 128], BF16)
        self.cmask = self.alloc("cmask", [TS], BF16)
        self.nmask = self.alloc("nmask", [NS], BF16)
        self.rcos = self.alloc("rcos", [TOKS], BF16)
        self.rsin = self.alloc("rsin", [TOKS], BF16)
        cdin = self.din
        P.dma("pool", self.rcos, cdin["c_rcos"][:, :], slot="pc0", writes=["rcos"])
        P.dma("pool", self.rsin, cdin["c_rsin"][:, :], slot="pc1", writes=["rsin"])
        P.dma("pool", self.band, cdin["c_band"][:, :, :], slot="pc2", writes=["band"])
        P.dma("pool", self.cmask, cdin["c_cmask"][:, :], slot="pc3", writes=["cmask"])
        P.dma("pool", self.nmask[0:NS], cdin["c_nmask"][:, :], slot="pc4", writes=["nmask"])
        P.dma("sp", self.X[0:NS, NT, :], self.din["xs"][:, :], slot=("x", NT), writes=[("X", NT)])
        xl = []
        for t in range(NT):
            xl.append(P.dma("sp", self.X[:, t, :], self.din["xp"][t * 128:(t + 1) * 128, :], slot=("x", t), writes=[("X", t)],
                            after=([xl[t - 4]] if t >= 4 else [])))
        self.plan_weights()
        only = self.only
        self.norm_setup()
        nn = lambda gi, le=False: (lambda t, part="ab": self.norm_tile(gi, t, part, lnexp=le))
        allt = list(range(NT)) + [NT]
        if "ret" in only:
            self.P.region = 2
            self.retention(nn(1))
            self.P.region = 0
        else:
            self.free("rcos", "rsin")
            for t in allt:
                self.norm_tile(1, t)
        if "dumpx" in self.dbg:
            for t in range(NT):
                P.dma("sp", self.dout["yp"][t * 128:(t + 1) * 128, :], self.X[:, t, :], slot="yo", reads=[("X", t)], new_gen=False)
        if "swa" in only:
            self.swa_consts()
        if "ffn0" in only:
            self.ffn(0, last=False, next_norm=nn(2))
        else:
            for t in allt:
                self.norm_tile(2, t)
        if "swa" in only:
            self.P.region = 1
            self.swa(nn(3, True))
            self.P.region = 0
        else:
            for t in allt:
                self.norm_tile(3, t)
        if "ffn1" in only:
            self.ffn(1, last=True)
        self.resolve_evictions()
        P.sched_regions = SCHED_REGIONS
        P.build()
        P.close()
        return self.nc


_CACHE = {}


def _get_program(stage=99):
    if stage not in _CACHE:
        consts = build_consts()
        m = Mega(stage=stage)
        nc = m.build(consts)
        _CACHE[stage] = (nc, consts)
    return _CACHE[stage]


def kernel(x_prompt, x_sample, state_ret, cache_swa_k, cache_swa_v, norm_mix, norm_ffn,
           w_ret_in, w_ret_out, w_swa_in, w_swa_out, swa_q_norm, swa_k_norm, swa_sinks,
           w_ffn_in, w_ffn_out, _stage=99):
    nc, consts = _get_program(_stage)
    f = lambda a: np.ascontiguousarray(np.asarray(a, dtype=np.float32))
    shared = {
        "norm_mix": f(norm_mix), "norm_ffn": f(norm_ffn),
        "w_ret_in": f(w_ret_in[0]), "w_ret_out": f(w_ret_out[0]), "w_swa_in": f(w_swa_in[0]), "w_swa_out": f(w_swa_out[0]),
        "swa_q_norm": f(swa_q_norm), "swa_k_norm": f(swa_k_norm), "swa_sinks": f(swa_sinks),
        "w_ffn_in0": f(w_ffn_in[0]), "w_ffn_in1": f(w_ffn_in[1]), "w_ffn_out0": f(w_ffn_out[0]), "w_ffn_out1": f(w_ffn_out[1]),
    }
    shared.update(consts)
    in_maps = []
    for c in range(8):
        m = dict(shared)
        m["xp"] = f(x_prompt[c])
        m["xs"] = f(x_sample[NB * c:NB * (c + 1)]).reshape(NS, D)
        m["state"] = f(state_ret[0, NB * c:NB * (c + 1)])
        m["ck"] = f(cache_swa_k[0, NB * c:NB * (c + 1)]).reshape(NB, 128, 256)
        m["cv"] = f(cache_swa_v[0, NB * c:NB * (c + 1)]).reshape(NB, 128, 256)
        in_maps.append(m)
    res = run_bass_kernel_spmd(nc, in_maps, core_ids=list(range(8)))
    rs = res.results
    y_p = np.stack([r["yp"] for r in rs], 0).astype(np.float32)
    y_s = np.concatenate([r["ys"].reshape(NB, TS, D) for r in rs], 0).astype(np.float32)
    st_p = np.stack([r["sp_out"] for r in rs], 0)[None].astype(np.float32)
    st_s = np.concatenate([r["ss_out"] for r in rs], 0)[None].astype(np.float32)
    k_p = np.stack([r["kp"].reshape(128, KVH, HD) for r in rs], 0)[None].astype(np.float32)
    v_p = np.stack([r["vp"].reshape(128, KVH, HD) for r in rs], 0)[None].astype(np.float32)
    k_s = np.concatenate([r["ks"].reshape(NB, 128, KVH, HD) for r in rs], 0)[None].astype(np.float32)
    v_s = np.concatenate([r["vs"].reshape(NB, 128, KVH, HD) for r in rs], 0)[None].astype(np.float32)
    return (y_p, y_s, st_p, st_s, k_p, v_p, k_s, v_s)
```
